# Optimizing a Trainium2 kernel written in Bass

```python
import jax, jax.numpy as jnp
from jax import lax
import numpy as np

D_MODEL = 1024
BATCH = 2
SEQ = 8192
DEPTH = 1

GRID_W = 64
CTX_LEN = 256
CONV_W = 1024
CONV_K = 3
N_HEADS = 16
HEAD_DIM = 64
ATTN_W = N_HEADS * HEAD_DIM
WIN_ROWS_MAX = 8
WIN_COLS = 16
ROPE_BASE = 10000.0
EPS = 1e-6
N_BRANCH = 2
IN_SIZES = [CONV_W] * 4 + [ATTN_W] * 4 + [D_MODEL] * N_BRANCH
IN_COLS = sum(IN_SIZES)
K_OFF = 4 * CONV_W + ATTN_W
V_END = 4 * CONV_W + 3 * ATTN_W

kernel_name = "hybrid_gated_conv_neighbourhood_attn_block"


def _rmsnorm(u, g):
    u32 = u.astype(jnp.float32)
    return (u32 * lax.rsqrt(jnp.mean(u32 * u32, axis=-1, keepdims=True) + EPS)).astype(u.dtype) * g


def _modulation(cond, w_mod, b_mod):
    m = jax.nn.silu(cond) @ w_mod + b_mod
    return jnp.split(m, 3, axis=-1)


def _split_in(p):
    idx = np.cumsum(IN_SIZES)[:-1].tolist()
    return jnp.split(p, idx, axis=-1)


def _dwconv3(u, w, b):
    L = u.shape[1]
    up = jnp.pad(u, ((0, 0), (1, 1), (0, 0)))
    return up[:, :L] * w[0] + up[:, 1:L + 1] * w[1] + up[:, 2:] * w[2] + b


def _rope_2d_tables(L, dtype):
    t = jnp.arange(L, dtype=jnp.int32)
    row = (t // GRID_W).astype(jnp.float32)
    col = (t % GRID_W).astype(jnp.float32)
    half = HEAD_DIM // 2
    inv = ROPE_BASE ** (-jnp.arange(0, half, 2, dtype=jnp.float32) / half)
    ang_r = row[:, None] * inv
    ang_c = col[:, None] * inv
    ang = jnp.concatenate([ang_r, ang_r, ang_c, ang_c], axis=-1)
    return jnp.cos(ang).astype(dtype), jnp.sin(ang).astype(dtype)


def _rot_half_axial(u):
    u1, u2, u3, u4 = jnp.split(u, 4, axis=-1)
    return jnp.concatenate([-u2, u1, -u4, u3], axis=-1)


def _apply_rope(u, cos, sin):
    return u * cos[None, :, None, :] + _rot_half_axial(u) * sin[None, :, None, :]


def _heads(u):
    return u.reshape(u.shape[0], u.shape[1], N_HEADS, HEAD_DIM)


def _neighbourhood_attention(q, k, v, k_ctx, v_ctx, rpb):
    bsz, L, H, Dh = q.shape
    rows = L // GRID_W
    wr = min(WIN_ROWS_MAX, rows)
    n_nb = wr * WIN_COLS
    scale = HEAD_DIM ** -0.5
    qg = q.reshape(bsz, rows, GRID_W, H, Dh)
    kg = k.reshape(bsz, rows, GRID_W, H, Dh)
    vg = v.reshape(bsz, rows, GRID_W, H, Dh)
    col = jnp.arange(GRID_W, dtype=jnp.int32)
    col_start = jnp.clip(col - WIN_COLS // 2, 0, GRID_W - WIN_COLS)
    col_idx = col_start[:, None] + jnp.arange(WIN_COLS, dtype=jnp.int32)
    dc_idx = col_idx - col[:, None] + (WIN_COLS - 1)

    def row_block(r):
        rs = jnp.clip(r - wr // 2, 0, rows - wr)
        q_r = lax.dynamic_index_in_dim(qg, r, axis=1, keepdims=False)
        k_slab = lax.dynamic_slice_in_dim(kg, rs, wr, axis=1)
        v_slab = lax.dynamic_slice_in_dim(vg, rs, wr, axis=1)
        k_win = k_slab[:, :, col_idx]
        v_win = v_slab[:, :, col_idx]
        dr_idx = rs + jnp.arange(wr, dtype=jnp.int32) - r + (WIN_ROWS_MAX - 1)
        bias = rpb[:, dr_idx[None, :, None], dc_idx[:, None, :]]
        s_nb = (jnp.einsum('bqhd,bpqjhd->bqhpj', q_r, k_win).astype(jnp.float32) * scale
                + jnp.transpose(bias, (1, 0, 2, 3))[None].astype(jnp.float32))
        s_nb = s_nb.reshape(bsz, GRID_W, H, n_nb)
        s_ctx = jnp.einsum('bqhd,bchd->bqhc', q_r, k_ctx).astype(jnp.float32) * scale
        p = jax.nn.softmax(jnp.concatenate([s_nb, s_ctx], axis=-1), axis=-1).astype(v.dtype)
        p_nb = p[..., :n_nb].reshape(bsz, GRID_W, H, wr, WIN_COLS)
        p_ctx = p[..., n_nb:]
        return (jnp.einsum('bqhpj,bpqjhd->bqhd', p_nb, v_win)
                + jnp.einsum('bqhc,bchd->bqhd', p_ctx, v_ctx))

    out = lax.map(row_block, jnp.arange(rows, dtype=jnp.int32))
    return jnp.transpose(out, (1, 0, 2, 3, 4)).reshape(bsz, L, H * Dh)


def _ctx_attention(q, k, v):
    scale = HEAD_DIM ** -0.5
    s = jnp.einsum('bqhd,bkhd->bhqk', q, k).astype(jnp.float32) * scale
    p = jax.nn.softmax(s, axis=-1).astype(v.dtype)
    o = jnp.einsum('bhqk,bkhd->bqhd', p, v)
    return o.reshape(o.shape[0], o.shape[1], ATTN_W)


def _mixer_out(parts, attn_o, conv_w, conv_b, w_out_conv, w_out_attn, w_o):
    b_gate, c_gate, x_in, z_a, _, _, _, z_b, g_a, g_b = parts
    y_a = (jax.nn.silu(z_a) * b_gate * _dwconv3(c_gate * x_in, conv_w, conv_b)) @ w_out_conv
    y_b = (jax.nn.silu(z_b) * attn_o) @ w_out_attn
    merged = jax.nn.sigmoid(g_a) * y_a + jax.nn.sigmoid(g_b) * y_b
    return merged @ w_o


def setup_inputs(seed: int = 0) -> dict:
    key = jax.random.key(seed)
    ks = jax.random.split(key, 16)
    f32 = jnp.float32
    nrm = lambda k, shape, s: jax.random.normal(k, shape, f32) * s
    return {
        "x": nrm(ks[0], (BATCH, SEQ, D_MODEL), 1.0),
        "c": nrm(ks[1], (BATCH, D_MODEL), 1.0),
        "ctx": nrm(ks[2], (BATCH, CTX_LEN, D_MODEL), 1.0),
        "c_ctx": nrm(ks[3], (D_MODEL,), 1.0),
        "w_mod": nrm(ks[4], (DEPTH, D_MODEL, 3 * D_MODEL), D_MODEL ** -0.5),
        "b_mod": nrm(ks[5], (DEPTH, 3 * D_MODEL), 0.02),
        "pre_g": 1.0 + nrm(ks[6], (DEPTH, D_MODEL), 0.05),
        "post_g": 1.0 + nrm(ks[7], (DEPTH, D_MODEL), 0.05),
        "w_in": nrm(ks[8], (DEPTH, D_MODEL, IN_COLS), D_MODEL ** -0.5),
        "conv_w": nrm(ks[9], (DEPTH, CONV_K, CONV_W), CONV_K ** -0.5),
        "conv_b": nrm(ks[10], (DEPTH, CONV_W), 0.02),
        "rpb": nrm(ks[11], (DEPTH, N_HEADS, 2 * WIN_ROWS_MAX - 1, 2 * WIN_COLS - 1), 0.1),
        "w_out_conv": nrm(ks[12], (DEPTH, CONV_W, D_MODEL), CONV_W ** -0.5),
        "w_out_attn": nrm(ks[13], (DEPTH, ATTN_W, D_MODEL), ATTN_W ** -0.5),
        "w_o": nrm(ks[14], (DEPTH, D_MODEL, D_MODEL), D_MODEL ** -0.5),
    }


def reference(x, c, ctx, c_ctx, w_mod, b_mod, pre_g, post_g, w_in, conv_w, conv_b, rpb,
              w_out_conv, w_out_attn, w_o):
    L = x.shape[1]
    cos, sin = _rope_2d_tables(L, x.dtype)
    for i in range(DEPTH):
        last = i == DEPTH - 1
        sh, sc, gt = _modulation(c, w_mod[i], b_mod[i])
        sh_c, sc_c, gt_c = _modulation(c_ctx, w_mod[i], b_mod[i])
        h = _rmsnorm(x, pre_g[i]) * (1.0 + sc[:, None, :]) + sh[:, None, :]
        hc = _rmsnorm(ctx, pre_g[i]) * (1.0 + sc_c) + sh_c
        parts = _split_in(h @ w_in[i])
        q = _apply_rope(_heads(parts[4]), cos, sin)
        k = _apply_rope(_heads(parts[5]), cos, sin)
        v = _heads(parts[6])
        if last:
            kv_c = hc @ w_in[i][:, K_OFF:V_END]
            k_c, v_c = jnp.split(kv_c, 2, axis=-1)
            k_c, v_c = _heads(k_c), _heads(v_c)
        else:
            parts_c = _split_in(hc @ w_in[i])
            k_c, v_c = _heads(parts_c[5]), _heads(parts_c[6])
        attn = _neighbourhood_attention(q, k, v, k_c, v_c, rpb[i])
        y = _mixer_out(parts, attn, conv_w[i], conv_b[i], w_out_conv[i], w_out_attn[i], w_o[i])
        if not last:
            attn_c = _ctx_attention(_heads(parts_c[4]), k_c, v_c)
            y_c = _mixer_out(parts_c, attn_c, conv_w[i], conv_b[i], w_out_conv[i], w_out_attn[i], w_o[i])
            ctx = ctx + gt_c * _rmsnorm(y_c, post_g[i])
        x = x + gt[:, None, :] * _rmsnorm(y, post_g[i])
    return x
```

```python
import numpy as np
from contextlib import ExitStack
import concourse.bass as bass
import concourse.mybir as mybir
from concourse.bass_utils import run_bass_kernel_spmd

F32 = mybir.dt.float32
BF16 = mybir.dt.bfloat16
ALU = mybir.AluOpType
AF = mybir.ActivationFunctionType

NCORES = 8
D = 1024
SEQ = 8192
GW = 64
CTX = 256
NH = 16
HD = 64
OWN = 2048
SLAB = 2560
KTW = SLAB + CTX
NEG = -30000.0
EPS = 1e-6
VP = 1
NVT = 22
ENGS = ("pe", "act", "dve", "pool", "sp")


class Buf:
    __slots__ = ("name", "lw", "rd", "dsem", "dcount")

    def __init__(self, name):
        self.name = name
        self.lw = None
        self.rd = []
        self.dsem = None
        self.dcount = 0


class Op:
    __slots__ = ("eng", "fn", "deps", "idx", "flag", "val", "waits", "dma_inc")

    def __init__(self, eng, fn):
        self.eng = eng
        self.fn = fn
        self.deps = []
        self.flag = False
        self.val = 0
        self.waits = []
        self.dma_inc = None


class Tracker:
    def __init__(self, nc, stack):
        self.nc = nc
        self.stack = stack
        self.ops = {e: [] for e in ENGS}
        self.esem = {e: stack.enter_context(nc.semaphore("es_" + e)) for e in ENGS}
        self.nsem = 0
        self.order = []
        self.dma_last = {}

    def new_sem(self, name):
        self.nsem += 1
        return self.stack.enter_context(self.nc.semaphore(f"ds_{name}_{self.nsem}"))

    def _collect(self, op, reads, writes):
        for b in reads:
            if b.lw is not None:
                op.deps.append(b.lw)
        for b in writes:
            if b.lw is not None:
                op.deps.append(b.lw)
            op.deps.extend(b.rd)

    def _add(self, o):
        o.idx = len(self.ops[o.eng])
        self.ops[o.eng].append(o)
        self.order.append(o)

    def op(self, eng, fn, reads=(), writes=()):
        o = Op(eng, fn)
        self._collect(o, reads, writes)
        self._add(o)
        ev = ("e", eng, o)
        for b in reads:
            b.rd.append(ev)
        for b in writes:
            b.lw = ev
            b.rd = []
        return o

    def dma(self, q, out, in_, reads=(), writes=(), sem_buf=None):
        sb = sem_buf if sem_buf is not None else (writes[0] if writes else reads[0])
        if sb.dsem is None:
            sb.dsem = self.new_sem(sb.name)
        sb.dcount += 1
        o = Op(q, lambda e, out=out, in_=in_: e.dma_start(out=out, in_=in_))
        self._collect(o, reads, writes)
        self._add(o)
        o.dma_inc = sb.dsem
        ev = ("d", sb.dsem, 16 * sb.dcount)
        self.dma_last[sb.dsem.num] = ev
        for b in reads:
            b.rd.append(ev)
        for b in writes:
            b.lw = ev
            b.rd = []
        return ev

    def wait_events(self, eng, evs):
        o = Op(eng, None)
        o.deps.extend(evs)
        self._add(o)

    def barrier(self):
        evs = list(self.dma_last.values())
        for e in ENGS:
            for o in reversed(self.ops[e]):
                if o.fn is not None and o.dma_inc is None:
                    evs.append(("e", e, o))
                    break
        for e in ENGS:
            self.wait_events(e, evs)

    def resolve(self):
        known = {e: {p: -1 for p in ENGS} for e in ENGS}
        dknown = {e: {} for e in ENGS}
        for o in self.order:
            need = {}
            dneed = {}
            for ev in o.deps:
                if ev[0] == "e":
                    _, p, po = ev
                    if p == "pe" and o.eng == "pe":
                        continue
                    if po.idx > need.get(p, (-1, None))[0]:
                        need[p] = (po.idx, po)
                else:
                    _, sem, val = ev
                    if val > dneed.get(sem.num, (0, None))[0]:
                        dneed[sem.num] = (val, sem)
            for p, (idx, po) in need.items():
                if known[o.eng][p] >= idx:
                    continue
                known[o.eng][p] = idx
                po.flag = True
                o.waits.append(("e", p, po))
            for num, (val, sem) in dneed.items():
                if dknown[o.eng].get(num, 0) >= val:
                    continue
                dknown[o.eng][num] = val
                o.waits.append(("d", sem, val))
        for e in ENGS:
            c = 0
            for o in self.ops[e]:
                if o.flag:
                    c += 1
                    o.val = c

    def emit(self):
        self.resolve()
        nc = self.nc
        with nc.Block() as block:
            def run(eng_name):
                def body(e):
                    for o in self.ops[eng_name]:
                        for w in o.waits:
                            if w[0] == "e":
                                e.wait_ge(self.esem[w[1]], w[2].val)
                            else:
                                e.wait_ge(w[1], w[2])
                        if o.fn is None:
                            continue
                        ins = o.fn(e)
                        if o.dma_inc is not None:
                            ins.then_inc(o.dma_inc, 16)
                        if o.flag:
                            ins.then_inc(self.esem[eng_name], 1)
                return body
            block.tensor(run("pe"))
            block.scalar(run("act"))
            block.vector(run("dve"))
            block.gpsimd(run("pool"))
            block.sync(run("sp"))


def build_nc(debug=False):
    nc = bass.Bass("TRN2", target_bir_lowering=False)

    def din(name, shape):
        return nc.dram_tensor(name, shape, F32, kind="ExternalInput").ap()

    xT = din("xT", [SLAB // 256, 128, 8 * 256])
    xo = din("xo", [OWN, D])
    ctxT = din("ctxT", [128, 8 * CTX])
    cvec = din("cvec", [128, 16])
    w_mod = din("w_mod", [D, 3 * D])
    bmod2 = din("bmod2", [128, 32])
    bgt = din("bgt", [128, D])
    preg2 = din("preg2", [128, 16])
    postg = din("postg", [128, D])
    wmain = din("wmain", [8, 128, 8 * 7 * 128])
    wv = din("wv", [8 // VP, 128, 8 * VP * 128])
    wg = din("wg", [8, 128, 8 * 2 * 128])
    woc = din("woc", [8, 128, 8 * 128])
    woa = din("woa", [8, 128, 8 * 128])
    wo = din("wo", [128, 8 * D])
    convw = din("convw", [128, 24])
    convb = din("convb", [128, 8])
    flags = din("flags", [128, 2])
    bias = din("bias", [8, 128, 5 * 2 * 5 * 128])
    cosT = din("cosT", [128, SLAB])
    sinT = din("sinT", [128, SLAB])
    rm = din("rm", [128, 128])
    ident = din("ident", [128, 128])
    out = nc.dram_tensor("out", [OWN, D], F32, kind="ExternalOutput").ap()
    dbg = {}
    if debug:
        for nm, shp in (("d_hT", [128, 8 * SLAB]), ("d_aT", [128, 8 * OWN]), ("d_cT", [128, 8 * OWN]),
                        ("d_mT", [128, 8 * OWN]), ("d_mod", [128, 32]), ("d_pgt", [128, D]),
                        ("d_qT", [128, OWN]), ("d_kT", [128, KTW])):
            dbg[nm] = nc.dram_tensor(nm, shp, F32, kind="ExternalOutput").ap()

    with ExitStack() as st:
        T = Tracker(nc, st)

        def sbt(stack, name, cols, dt):
            return stack.enter_context(nc.sbuf_tensor(name, [128, cols], dt))

        Q = [st.enter_context(nc.psum_tensor(f"Q{i}", [128, 1024], F32)) for i in range(4)]
        BK = [Buf(f"bk{i}") for i in range(8)]

        def bank(i):
            return Q[i // 2][:, (i % 2) * 512:(i % 2 + 1) * 512]

        rr = [0]

        def nextbank():
            i = rr[0] % 8
            rr[0] += 1
            return i

        def MM(o, l, r, start, stop, reads, writes):
            T.op("pe", lambda e, o=o, l=l, r=r, s=start, p=stop: e.matmul(o, lhsT=l, rhs=r, start=s, stop=p),
                 reads, writes)

        def ACT(o, i, func, reads, writes, bias=None, scale=None, accum=None):
            kw = {}
            if bias is not None:
                kw["bias"] = bias
            if scale is not None:
                kw["scale"] = scale
            if accum is not None:
                kw["accum_out"] = accum
            T.op("act", lambda e, o=o, i=i, f=func, kw=kw: e.activation(out=o, in_=i, func=f, **kw), reads, writes)

        def TTo(eng, o, a, b, op, reads, writes):
            T.op(eng, lambda e, o=o, a=a, b=b, op=op: e.tensor_tensor(out=o, in0=a, in1=b, op=op), reads, writes)

        def TS(eng, o, a, s1, s2, op0, op1, reads, writes):
            if s2 is None:
                T.op(eng, lambda e, o=o, a=a, s1=s1, op0=op0: e.tensor_scalar(out=o, in0=a, scalar1=s1, scalar2=None,
                                                                             op0=op0), reads, writes)
            else:
                T.op(eng, lambda e, o=o, a=a, s1=s1, s2=s2, op0=op0, op1=op1: e.tensor_scalar(
                    out=o, in0=a, scalar1=s1, scalar2=s2, op0=op0, op1=op1), reads, writes)

        def STT(eng, o, a, s, b, op0, op1, reads, writes):
            T.op(eng, lambda e, o=o, a=a, s=s, b=b, op0=op0, op1=op1: e.scalar_tensor_tensor(
                out=o, in0=a, scalar=s, in1=b, op0=op0, op1=op1), reads, writes)

        def CP(eng, o, i, reads, writes):
            T.op(eng, lambda e, o=o, i=i: e.tensor_copy(out=o, in_=i), reads, writes)

        def RECIP(o, i, reads, writes):
            T.op("dve", lambda e, o=o, i=i: e.reciprocal(out=o, in_=i), reads, writes)

        def MEMSET(eng, o, v, writes):
            T.op(eng, lambda e, o=o, v=v: e.memset(o, v), (), writes)

        def dump(name, src, B):
            if debug and name in dbg:
                T.dma("pool", dbg[name], src, reads=[B], sem_buf=B_dbg)

        B_dbg = Buf("dbg")

        ident_f = sbt(st, "ident_f", 128, F32); B_identf = Buf("identf")
        ident_b = sbt(st, "ident_b", 128, BF16); B_identb = Buf("identb")
        rm_b = sbt(st, "rm_b", 128, BF16); B_rm = Buf("rm")
        ones_b = sbt(st, "ones_b", 128, BF16); B_ones = Buf("ones")
        cos_s = sbt(st, "cos_s", SLAB, BF16); B_cos = Buf("cos")
        sin_s = sbt(st, "sin_s", SLAB, BF16); B_sin = Buf("sin")
        pgt = sbt(st, "pgt", D, F32); B_pgt = Buf("pgt")
        modsb = sbt(st, "modsb", 32, F32); B_mod = Buf("mod")
        gs = sbt(st, "gs", 16, F32); B_gs = Buf("gs")
        convw_s = sbt(st, "convw_s", 24, F32); B_cw = Buf("cw")
        convb_s = sbt(st, "convb_s", 8, F32); B_cb = Buf("cb")
        flags_s = sbt(st, "flags_s", 2, F32); B_fl = Buf("fl")
        Bc = Buf("consts")

        T.dma("sp", ident_f[:], ident, writes=[B_identf])
        T.dma("pool", ident_b[:], ident, writes=[B_identb])
        T.dma("pool", rm_b[:], rm, writes=[B_rm])
        T.dma("pool", cos_s[:], cosT, writes=[B_cos])
        T.dma("pool", sin_s[:], sinT, writes=[B_sin])
        T.dma("sp", convw_s[:], convw, writes=[B_cw])
        T.dma("sp", convb_s[:], convb, writes=[B_cb])
        T.dma("sp", flags_s[:], flags, writes=[B_fl])
        MEMSET("pool", ones_b[:], 1.0, [B_ones])

        with ExitStack() as sA:
            hT = sbt(sA, "hT", 8 * SLAB, BF16); B_hT = Buf("hT")
            hcT = sbt(sA, "hcT", 8 * CTX, BF16); B_hcT = Buf("hcT")
            aT = sbt(sA, "aT", 8 * OWN, BF16); B_aT = [Buf(f"aT{i}") for i in range(8)]
            cT = sbt(sA, "cT", 8 * OWN, BF16); B_cT = [Buf(f"cT{i}") for i in range(8)]
            wmc = sbt(sA, "wmc", 8 * 4 * 128, BF16); B_wmc = Buf("wmc")
            wv_s = sbt(sA, "wv_s", 8 * VP * 128, BF16); B_wv = Buf("wv")
            wmain_v = [wmain[cc].rearrange("p (kc j n) -> p kc j n", kc=8, j=7) for cc in range(8)]

            with ExitStack() as s0:
                cv = sbt(s0, "cv", 16, F32); B_cv = Buf("cv")
                sl = sbt(s0, "sl", 16, BF16); B_sl = Buf("sl")
                ones_f = sbt(s0, "ones_f", 128, F32); B_onesf = Buf("onesf")
                srep = sbt(s0, "srep", 8 * 128, BF16); B_srep = Buf("srep")
                wmb = sbt(s0, "wmb", 8 * 1024, BF16); B_wmb = Buf("wmb")
                bmod_s = sbt(s0, "bmod_s", 32, F32); B_bm = Buf("bm")
                preg_s = sbt(s0, "preg_s", 16, F32); B_pg = Buf("pg")
                bgt_s = sbt(s0, "bgt_s", D, F32); B_bgt = Buf("bgt")
                postg_s = sbt(s0, "postg_s", D, F32); B_pog = Buf("pog")
                tmp16 = sbt(s0, "tmp16", 16, F32); B_t16 = Buf("t16")
                NG = 256
                xt = [sbt(s0, f"xt{i}", 8 * NG, F32) for i in range(4)]; B_xt = [Buf(f"xt{i}") for i in range(4)]
                sq = [sbt(s0, f"sq{i}", 8 * NG, BF16) for i in range(2)]; B_sq = [Buf(f"sq{i}") for i in range(2)]
                rstd = [sbt(s0, f"rstd{i}", NG, F32) for i in range(3)]; B_rstd = [Buf(f"rstd{i}") for i in range(3)]
                tmpx = [sbt(s0, f"tmpx{i}", NG, F32) for i in range(4)]; B_tmpx = [Buf(f"tmpx{i}") for i in range(4)]

                T.dma("sp", cv[:], cvec, writes=[B_cv])
                T.dma("sp", bmod_s[:], bmod2, writes=[B_bm])
                T.dma("sp", preg_s[:], preg2, writes=[B_pg])
                T.dma("sp", bgt_s[:], bgt, writes=[B_bgt])
                T.dma("sp", postg_s[:], postg, writes=[B_pog])
                wmr = w_mod.rearrange("(kc p) n -> p kc n", p=128)
                T.dma("pool", wmb[:].rearrange("p (kc n) -> p kc n", kc=8), wmr[:, :, 0:1024], writes=[B_wmb])
                MEMSET("pool", ones_f[:], 1.0, [B_onesf])
                ACT(sl[:], cv[:], AF.Silu, [B_cv], [B_sl])
                for kc in range(8):
                    TS("dve", srep[:, kc * 128:(kc + 1) * 128], ones_f[:], sl[:, kc * 2:kc * 2 + 1], None, ALU.mult, None,
                       [B_onesf, B_sl], [B_srep])

                groups = [(xT[g], g * NG, NG, hT, SLAB, g * NG, 0, B_hT) for g in range(SLAB // NG)]
                groups.append((ctxT, 0, CTX, hcT, CTX, 0, 1, B_hcT))

                stat_bank = {}
                eps_s = sbt(s0, "eps_s", 1, F32); B_eps = Buf("eps")
                MEMSET("pool", eps_s[:], EPS, [B_eps])

                def stats(gi):
                    src, t0, n, dst, dw, d0, tsel, Bd = groups[gi]
                    x_ = xt[gi % 4]
                    Bx = B_xt[gi % 4]
                    T.dma("sp", x_[:, 0:8 * n], src, writes=[Bx])
                    sq_ = sq[gi % 2]
                    ACT(sq_[:, 0:8 * n], x_[:, 0:8 * n], AF.Square, [Bx], [B_sq[gi % 2]])
                    bi = nextbank()
                    for kc in range(8):
                        MM(bank(bi)[:, 0:n], ones_b[:], sq_[:, kc * n:(kc + 1) * n], kc == 0, kc == 7,
                           [B_ones, B_sq[gi % 2]], [BK[bi]])
                    stat_bank[gi] = bi

                def fin(gi):
                    n = groups[gi][2]
                    bi = stat_bank[gi]
                    r_ = rstd[gi % 3]
                    Br = B_rstd[gi % 3]
                    ACT(r_[:, 0:n], bank(bi)[:, 0:n], AF.Ln, [BK[bi], B_eps], [Br], bias=eps_s[:, 0:1], scale=1.0 / D)
                    ACT(r_[:, 0:n], r_[:, 0:n], AF.Exp, [Br], [Br], scale=-0.5)

                def apply(gi):
                    src, t0, n, dst, dw, d0, tsel, Bd = groups[gi]
                    x_ = xt[gi % 4]
                    Bx = B_xt[gi % 4]
                    r_ = rstd[gi % 3]
                    Br = B_rstd[gi % 3]
                    for kc in range(8):
                        tx = tmpx[kc % 4]
                        Bt = B_tmpx[kc % 4]
                        STT("dve", tx[:, 0:n], x_[:, kc * n:(kc + 1) * n], gs[:, kc * 2 + tsel:kc * 2 + tsel + 1],
                            r_[:, 0:n], ALU.mult, ALU.mult, [Bx, B_gs, Br], [Bt])
                        ACT(dst[:, kc * dw + d0:kc * dw + d0 + n], tx[:, 0:n], AF.Identity, [Bt, B_mod], [Bd],
                            bias=modsb[:, kc * 2 + tsel:kc * 2 + tsel + 1], scale=1.0)

                stats(0)
                fin(0)
                bmod_i = nextbank()

                def modmm(j, off):
                    for kc in range(8):
                        MM(bank(bmod_i)[:, j * 2:j * 2 + 2], wmb[:, kc * 1024 + off:kc * 1024 + off + 128],
                           sl[:, kc * 2:kc * 2 + 2], kc == 0, kc == 7, [B_wmb, B_sl], [BK[bmod_i]])
                for j in range(8):
                    modmm(j, j * 128)
                stats(1)
                fin(1)
                T.dma("pool", wmb[:].rearrange("p (kc n) -> p kc n", kc=8), wmr[:, :, 1024:2048], writes=[B_wmb])
                for j in range(8, 16):
                    modmm(j, (j - 8) * 128)
                T.dma("pool", wmb[:].rearrange("p (kc n) -> p kc n", kc=8), wmr[:, :, 2048:3072], writes=[B_wmb])
                T.dma("pool", wv_s[:], wv[0], writes=[B_wv])
                T.dma("pool", wmc[:].rearrange("p (kc j n) -> p kc j n", kc=8, j=4), wmain_v[0][:, :, 0:4, :],
                      writes=[B_wmc])
                TTo("dve", modsb[:], bank(bmod_i)[:, 0:32], bmod_s[:], ALU.add, [BK[bmod_i], B_bm], [B_mod])
                TS("dve", tmp16[:], modsb[:, 16:32], 1.0, None, ALU.add, None, [B_mod], [B_t16])
                TTo("dve", gs[:], tmp16[:], preg_s[:], ALU.mult, [B_t16, B_pg], [B_gs])
                for half in range(2):
                    bi = nextbank()
                    for kc in range(8):
                        MM(bank(bi), srep[:, kc * 128:(kc + 1) * 128],
                           wmb[:, kc * 1024 + half * 512:kc * 1024 + (half + 1) * 512],
                           kc == 0, kc == 7, [B_srep, B_wmb], [BK[bi]])
                    TTo("dve", pgt[:, half * 512:(half + 1) * 512], bank(bi), bgt_s[:, half * 512:(half + 1) * 512],
                        ALU.add, [BK[bi], B_bgt], [B_pgt])
                TTo("dve", pgt[:], pgt[:], postg_s[:], ALU.mult, [B_pgt, B_pog], [B_pgt])
                ng = len(groups)
                for gi in range(ng):
                    if gi + 2 < ng:
                        stats(gi + 2)
                    apply(gi)
                    if gi + 2 < ng:
                        fin(gi + 2)
                if debug:
                    dump("d_pgt", pgt[:], B_pgt)
                    dump("d_hT", hT[:], B_hT)
                    dump("d_mod", modsb[:], B_mod)
                T.barrier()

            with ExitStack() as s2:
                wma = sbt(s2, "wma", 8 * 3 * 128, BF16); B_wma = Buf("wma")
                V_s = sbt(s2, "V_s", NVT * VP * 2 * 65, BF16); B_V = Buf("V")
                u_s = sbt(s2, "u_s", OWN + 2, BF16); B_u = Buf("u")
                bz_s = sbt(s2, "bz_s", OWN, BF16); B_bz = Buf("bz")
                acc = [sbt(s2, f"acc{i}", 512, F32) for i in range(2)]; B_acc = [Buf(f"acc{i}") for i in range(2)]
                xi_s = sbt(s2, "xi_s", 512, F32); B_xi = Buf("xi")
                sz_s = sbt(s2, "sz_s", 512, F32); B_sz = Buf("sz")
                hal_s = sbt(s2, "hal_s", 4, F32); B_hal = Buf("hal")
                qTz = [sbt(s2, f"qTz{i}", OWN, BF16) for i in range(2)]; B_qT = Buf("qT")
                kT = sbt(s2, "kT", KTW, BF16); B_kT = Buf("kT")
                szb = sbt(s2, "szb", OWN, BF16); B_szb = Buf("szb")
                qraw = sbt(s2, "qraw", 512, BF16); B_qraw = Buf("qraw")
                t1 = sbt(s2, "t1", 512, F32); B_t1 = Buf("t1")
                t2 = sbt(s2, "t2", 512, F32); B_t2 = Buf("t2")
                bias_s = sbt(s2, "bias_s", 5 * 2 * 5 * 128, BF16); B_bias = Buf("bias")
                PT = [[sbt(s2, f"PT{i}_{h}", 896, BF16) for h in range(2)] for i in range(2)]
                B_PT = [[Buf(f"PT{i}_{h}") for h in range(2)] for i in range(2)]
                print("SBUF bytes remaining at phase-2 peak:", nc.sbuf_bytes_remaining)
                attn_n = [sbt(s2, f"attn_n{i}", 128, F32) for i in range(2)]; B_an = [Buf(f"an{i}") for i in range(2)]
                rden = [sbt(s2, f"rden{i}", 2, F32) for i in range(2)]; B_rd = [Buf(f"rd{i}") for i in range(2)]

                MEMSET("pool", V_s[:], 1.0, [B_V])
                for i in range(2):
                    MEMSET("pool", qTz[i][:], 0.0, [B_qT])

                def load_wmc(c):
                    T.dma("pool", wmc[:].rearrange("p (kc j n) -> p kc j n", kc=8, j=4), wmain_v[c][:, :, 0:4, :],
                          writes=[B_wmc])

                def load_wma(c):
                    T.dma("pool", wma[:].rearrange("p (kc j n) -> p kc j n", kc=8, j=3), wmain_v[c][:, :, 4:7, :],
                          writes=[B_wma])

                def load_bias(c):
                    T.dma("pool", bias_s[:], bias[c], writes=[B_bias])

                def exp_bias():
                    ACT(bias_s[:], bias_s[:], AF.Exp, [B_bias], [B_bias])

                def load_wv(c):
                    T.dma("pool", wv_s[:], wv[c // VP], writes=[B_wv])

                load_wma(0)
                load_bias(0)
                for cc in range(8):
                    pl = cc % VP
                    if pl == 0:
                        for vt in range(NVT):
                            bi = nextbank()
                            for kc in range(8):
                                if vt < 20:
                                    l = hT[:, kc * SLAB + vt * 128:kc * SLAB + (vt + 1) * 128]
                                    rdl = [B_hT, B_wv]
                                else:
                                    l = hcT[:, kc * CTX + (vt - 20) * 128:kc * CTX + (vt - 19) * 128]
                                    rdl = [B_hcT, B_wv]
                                MM(bank(bi)[:, 0:VP * 128], l, wv_s[:, kc * VP * 128:(kc + 1) * VP * 128],
                                   kc == 0, kc == 7, rdl, [BK[bi]])
                            ov = V_s[:, vt * VP * 130:(vt + 1) * VP * 130].rearrange("p (h c) -> p h c", c=65)[:, :, 0:64]
                            iv = bank(bi)[:, 0:VP * 128].rearrange("p (h c) -> p h c", c=64)
                            ACT(ov, iv, AF.Copy, [BK[bi]], [B_V])
                        if cc + VP < 8:
                            load_wv(cc + VP)

                    def wblk(w, nj, kc, j):
                        return w[:, (kc * nj + j) * 128:(kc * nj + j + 1) * 128]

                    for g in range(4):
                        tok0 = 256 + g * 512
                        bs = [nextbank() for _ in range(4)]
                        for j in range(4):
                            for kc in range(8):
                                MM(bank(bs[j]), wblk(wmc, 4, kc, j), hT[:, kc * SLAB + tok0:kc * SLAB + tok0 + 512],
                                   kc == 0, kc == 7, [B_wmc, B_hT], [BK[bs[j]]])
                        ACT(xi_s[:], bank(bs[2]), AF.Copy, [BK[bs[2]]], [B_xi])
                        TTo("dve", u_s[:, 1 + g * 512:1 + (g + 1) * 512], bank(bs[1]), xi_s[:], ALU.mult,
                            [BK[bs[1]], B_xi], [B_u])
                        ACT(sz_s[:], bank(bs[3]), AF.Silu, [BK[bs[3]]], [B_sz])
                        TTo("dve", bz_s[:, g * 512:(g + 1) * 512], bank(bs[0]), sz_s[:], ALU.mult,
                            [BK[bs[0]], B_sz], [B_bz])
                    bi = nextbank()
                    for hi, tk in enumerate((255, 2304)):
                        for jj, j in enumerate((1, 2)):
                            col = hi * 2 + jj
                            for kc in range(8):
                                MM(bank(bi)[:, col:col + 1], wblk(wmc, 4, kc, j), hT[:, kc * SLAB + tk:kc * SLAB + tk + 1],
                                   kc == 0, kc == 7, [B_wmc, B_hT], [BK[bi]])
                    ACT(hal_s[:], bank(bi)[:, 0:4], AF.Copy, [BK[bi]], [B_hal])
                    for hi, ucol in enumerate((0, OWN + 1)):
                        STT("dve", u_s[:, ucol:ucol + 1], hal_s[:, hi * 2:hi * 2 + 1], flags_s[:, hi:hi + 1],
                            hal_s[:, hi * 2 + 1:hi * 2 + 2], ALU.mult, ALU.mult, [B_hal, B_fl], [B_u])
                    for pc in range(4):
                        s_ = pc * 512
                        a_ = acc[pc % 2]
                        Ba = B_acc[pc % 2]
                        TS("dve", a_[:], u_s[:, 1 + s_:1 + s_ + 512], convw_s[:, cc * 3 + 1:cc * 3 + 2],
                           convb_s[:, cc:cc + 1], ALU.mult, ALU.add, [B_u, B_cw, B_cb], [Ba])
                        STT("dve", a_[:], u_s[:, s_:s_ + 512], convw_s[:, cc * 3:cc * 3 + 1], a_[:], ALU.mult, ALU.add,
                            [B_u, B_cw, Ba], [Ba])
                        STT("dve", a_[:], u_s[:, 2 + s_:2 + s_ + 512], convw_s[:, cc * 3 + 2:cc * 3 + 3], a_[:],
                            ALU.mult, ALU.add, [B_u, B_cw, Ba], [Ba])
                        TTo("dve", aT[:, cc * OWN + s_:cc * OWN + s_ + 512], a_[:], bz_s[:, s_:s_ + 512], ALU.mult,
                            [Ba, B_bz], [B_aT[cc]])

                    if cc + 1 < 8:
                        load_wmc(cc + 1)

                    def rope_p1(j, tok0, scl):
                        ba = nextbank()
                        for kc in range(8):
                            MM(bank(ba), wblk(wma, 3, kc, j), hT[:, kc * SLAB + tok0:kc * SLAB + tok0 + 512],
                               kc == 0, kc == 7, [B_wma, B_hT], [BK[ba]])
                        ACT(qraw[:], bank(ba), AF.Copy, [BK[ba]], [B_qraw], scale=scl)
                        STT("dve", t1[:], bank(ba), scl, cos_s[:, tok0:tok0 + 512], ALU.mult, ALU.mult,
                            [BK[ba], B_cos, B_qraw], [B_t1])

                    def rope_p2(tok0, dst, d0, Bd):
                        bb = nextbank()
                        MM(bank(bb), rm_b[:], qraw[:], True, True, [B_rm, B_qraw], [BK[bb]])
                        TTo("dve", t2[:], bank(bb), sin_s[:, tok0:tok0 + 512], ALU.mult, [BK[bb], B_sin], [B_t2])
                        if dst is None:
                            for hh in range(2):
                                TTo("dve", qTz[hh][hh * 64:(hh + 1) * 64, d0:d0 + 512], t1[hh * 64:(hh + 1) * 64, :],
                                    t2[hh * 64:(hh + 1) * 64, :], ALU.add, [B_t1, B_t2], [Bd])
                        else:
                            TTo("dve", dst[:, d0:d0 + 512], t1[:], t2[:], ALU.add, [B_t1, B_t2], [Bd])

                    def zb_group(g):
                        ba = nextbank()
                        tok0 = 256 + g * 512
                        for kc in range(8):
                            MM(bank(ba), wblk(wma, 3, kc, 2), hT[:, kc * SLAB + tok0:kc * SLAB + tok0 + 512],
                               kc == 0, kc == 7, [B_wma, B_hT], [BK[ba]])
                        ACT(szb[:, g * 512:(g + 1) * 512], bank(ba), AF.Silu, [BK[ba]], [B_szb])

                    def ctxk_group():
                        ba = nextbank()
                        for kc in range(8):
                            MM(bank(ba)[:, 0:CTX], wblk(wma, 3, kc, 1), hcT[:, kc * CTX:(kc + 1) * CTX], kc == 0, kc == 7,
                               [B_wma, B_hcT], [BK[ba]])
                        ACT(kT[:, SLAB:KTW], bank(ba)[:, 0:CTX], AF.Copy, [BK[ba]], [B_kT])

                    fillers = [lambda g=g: zb_group(g) for g in range(4)] + [ctxk_group]
                    rjobs = [(0, 256 + g * 512, None, g * 512, B_qT, 0.125) for g in range(4)]
                    rjobs += [(1, g * 512, kT, g * 512, B_kT, 1.0) for g in range(5)]
                    for i, (j, tok0, dst, d0, Bd, scl) in enumerate(rjobs):
                        rope_p1(j, tok0, scl)
                        if i < len(fillers):
                            fillers[i]()
                        rope_p2(tok0, dst, d0, Bd)
                    if cc + 1 < 8:
                        load_wma(cc + 1)

                    exp_bias()
                    def emit_S(t):
                        var = {0: 1, 1: 2, 14: 3, 15: 4}.get(t, 0)
                        for hl in range(2):
                            S = Q[hl]
                            BS = [BK[2 * hl], BK[2 * hl + 1]]
                            qv = qTz[hl][:, t * 128:(t + 1) * 128]
                            for jb in range(5):
                                kt0 = (t + jb) * 128
                                Bb = BS[jb // 4]
                                MM(S[:, jb * 128:(jb + 1) * 128], kT[:, kt0:kt0 + 128], qv, True, True,
                                   [B_kT, B_qT], [Bb])
                            for cb in range(2):
                                MM(S[:, (5 + cb) * 128:(6 + cb) * 128], kT[:, SLAB + cb * 128:SLAB + (cb + 1) * 128],
                                   qv, True, True, [B_kT, B_qT], [BS[1]])
                            ACT(PT[t % 2][hl][:, 0:896], S[:, 0:896], AF.Exp, [BS[0], BS[1]], [B_PT[t % 2][hl]])
                            bo = (var * 2 + hl) * 5 * 128
                            TTo("dve", PT[t % 2][hl][:, 0:640], PT[t % 2][hl][:, 0:640], bias_s[:, bo:bo + 640], ALU.mult,
                                [B_PT[t % 2][hl], B_bias], [B_PT[t % 2][hl]])

                    def emit_PV(t):
                        ob = 4 + (t % 2)
                        for hl in range(2):
                            for blk in range(7):
                                vt = (t + blk) if blk < 5 else (20 + blk - 5)
                                vo = ((vt * VP + pl) * 2 + hl) * 65
                                MM(bank(ob)[:, hl * 65:(hl + 1) * 65], PT[t % 2][hl][:, blk * 128:(blk + 1) * 128],
                                   V_s[:, vo:vo + 65], blk == 0, blk == 6, [B_PT[t % 2][hl], B_V], [BK[ob]])

                    def emit_N(t):
                        ob = 4 + (t % 2)
                        rd_ = rden[t % 2]
                        an_ = attn_n[t % 2]
                        for hl in range(2):
                            RECIP(rd_[:, hl:hl + 1], bank(ob)[:, hl * 65 + 64:hl * 65 + 65], [BK[ob]], [B_rd[t % 2]])
                        for hl in range(2):
                            TS("dve", an_[:, hl * 64:(hl + 1) * 64], bank(ob)[:, hl * 65:hl * 65 + 64], rd_[:, hl:hl + 1],
                               None, ALU.mult, None, [BK[ob], B_rd[t % 2]], [B_an[t % 2]])

                    def emit_X(t):
                        tb = 6 + (t % 2)
                        an_ = attn_n[t % 2]
                        T.op("pe", lambda e, o=bank(tb)[:, 0:128], i=an_[:], idn=ident_f[:]: e.transpose(o, i, idn),
                             [B_an[t % 2], B_identf], [BK[tb]])
                        TTo("dve", cT[:, cc * OWN + t * 128:cc * OWN + (t + 1) * 128], bank(tb)[:, 0:128],
                            szb[:, t * 128:(t + 1) * 128], ALU.mult, [BK[tb], B_szb], [B_cT[cc]])

                    emit_S(0)
                    for t in range(16):
                        if t + 1 < 16:
                            emit_S(t + 1)
                        emit_PV(t)
                        if t > 0:
                            emit_X(t - 1)
                        emit_N(t)
                    emit_X(15)
                    if cc + 1 < 8:
                        load_bias(cc + 1)
                if debug:
                    dump("d_aT", aT[:], B_aT[7])
                    dump("d_cT", cT[:], B_cT[7])
                T.barrier()

            with ExitStack() as s3:
                mT = sbt(s3, "mT", 8 * OWN, BF16); B_mT = [Buf(f"mT{i}") for i in range(8)]
                wo_s = sbt(s3, "wo_s", 8 * D, BF16); B_wo = Buf("wo")
                T.dma("pool", wo_s[:], wo, writes=[B_wo])
                with ExitStack() as s3b:
                    w3 = [[sbt(s3b, f"w3_{i}_{k}", 8 * 128, BF16) for k in range(2)] for i in range(2)]
                    B_w3 = [[Buf(f"w3_{i}_{k}") for k in range(2)] for i in range(2)]
                    wg_s = [sbt(s3b, f"wg_s{i}", 8 * 2 * 128, BF16) for i in range(2)]; B_wg = [Buf(f"wg{i}") for i in range(2)]
                    sg = [sbt(s3b, f"sg{i}", 512, F32) for i in range(2)]; B_sg = [Buf(f"sg{i}") for i in range(2)]
                    m12 = [sbt(s3b, f"m12_{i}", 512, F32) for i in range(2)]; B_m12 = [Buf(f"m12_{i}") for i in range(2)]
                    def load_w3(o):
                        k = o % 2
                        T.dma("pool", w3[k][0][:], woc[o], writes=[B_w3[k][0]])
                        T.dma("pool", w3[k][1][:], woa[o], writes=[B_w3[k][1]])
                        T.dma("pool", wg_s[k][:], wg[o], writes=[B_wg[k]])
                    load_w3(0)
                    for oc in range(8):
                        sl_ = oc % 2
                        if oc + 1 < 8:
                            load_w3(oc + 1)
                        for g in range(4):
                            tok0 = 256 + g * 512
                            bs = [nextbank() for _ in range(4)]
                            for chc in range(8):
                                MM(bank(bs[0]), w3[sl_][0][:, chc * 128:(chc + 1) * 128],
                                   aT[:, chc * OWN + g * 512:chc * OWN + (g + 1) * 512], chc == 0, chc == 7,
                                   [B_w3[sl_][0], B_aT[chc]], [BK[bs[0]]])
                            for chc in range(8):
                                MM(bank(bs[1]), w3[sl_][1][:, chc * 128:(chc + 1) * 128],
                                   cT[:, chc * OWN + g * 512:chc * OWN + (g + 1) * 512], chc == 0, chc == 7,
                                   [B_w3[sl_][1], B_cT[chc]], [BK[bs[1]]])
                            for j in range(2):
                                for kc in range(8):
                                    MM(bank(bs[2 + j]), wg_s[sl_][:, (kc * 2 + j) * 128:(kc * 2 + j + 1) * 128],
                                       hT[:, kc * SLAB + tok0:kc * SLAB + tok0 + 512], kc == 0, kc == 7,
                                       [B_wg[sl_], B_hT], [BK[bs[2 + j]]])
                            for j in range(2):
                                ACT(sg[j][:], bank(bs[2 + j]), AF.Sigmoid, [BK[bs[2 + j]]], [B_sg[j]])
                                TTo("dve", m12[j][:], bank(bs[j]), sg[j][:], ALU.mult, [BK[bs[j]], B_sg[j]], [B_m12[j]])
                            TTo("dve", mT[:, oc * OWN + g * 512:oc * OWN + (g + 1) * 512], m12[0][:], m12[1][:], ALU.add,
                                [B_m12[0], B_m12[1]], [B_mT[oc]])
                    if debug:
                        dump("d_mT", mT[:], B_mT[7])
                    T.barrier()

                with ExitStack() as s4:
                    xtile = [sbt(s4, f"xtile{i}", D, F32) for i in range(2)]; B_xtile = [Buf(f"xtile{i}") for i in range(2)]
                    otile = [sbt(s4, f"otile{i}", D, F32) for i in range(2)]; B_ot = [Buf(f"ot{i}") for i in range(2)]
                    yg = [sbt(s4, f"yg{i}", D, F32) for i in range(2)]; B_yg = [Buf(f"yg{i}") for i in range(2)]
                    junk = sbt(s4, "junk", 512, F32); B_junk = Buf("junk")
                    ssq = [sbt(s4, f"ssq{i}", 4, F32) for i in range(2)]; B_ssq = [Buf(f"ssq{i}") for i in range(2)]
                    store_evs = []
                    T.dma("sp", xtile[0][:], xo[0:128, :], writes=[B_xtile[0]])
                    for t in range(16):
                        k_ = t % 2
                        if t + 1 < 16:
                            T.dma("sp", xtile[1 - k_][:], xo[(t + 1) * 128:(t + 2) * 128, :], writes=[B_xtile[1 - k_]])
                        qi = t % 4
                        Y = Q[qi]
                        BY = [BK[2 * qi], BK[2 * qi + 1]]
                        for half in range(2):
                            for oc in range(8):
                                MM(Y[:, half * 512:(half + 1) * 512], mT[:, oc * OWN + t * 128:oc * OWN + (t + 1) * 128],
                                   wo_s[:, oc * D + half * 512:oc * D + (half + 1) * 512], oc == 0, oc == 7,
                                   [B_mT[oc], B_wo], [BY[half]])
                        s_ = ssq[k_]
                        Bs = B_ssq[k_]
                        for half in range(2):
                            ACT(junk[:], Y[:, half * 512:(half + 1) * 512], AF.Square, [BY[half], Bs], [B_junk, Bs],
                                accum=s_[:, half:half + 1])
                        TTo("dve", s_[:, 2:3], s_[:, 0:1], s_[:, 1:2], ALU.add, [Bs], [Bs])
                        TS("dve", s_[:, 2:3], s_[:, 2:3], 1.0 / D, EPS, ALU.mult, ALU.add, [Bs], [Bs])
                        ACT(s_[:, 2:3], s_[:, 2:3], AF.Sqrt, [Bs], [Bs])
                        RECIP(s_[:, 3:4], s_[:, 2:3], [Bs], [Bs])
                        for half in range(2):
                            TTo("dve", yg[k_][:, half * 512:(half + 1) * 512], Y[:, half * 512:(half + 1) * 512],
                                pgt[:, half * 512:(half + 1) * 512], ALU.mult, [BY[half], B_pgt], [B_yg[k_]])
                        STT("dve", otile[k_][:], yg[k_][:], s_[:, 3:4], xtile[k_][:], ALU.mult, ALU.add,
                            [B_yg[k_], Bs, B_xtile[k_]], [B_ot[k_]])
                        ev = T.dma("sp", out[t * 128:(t + 1) * 128, :], otile[k_][:], reads=[B_ot[k_]])
                        store_evs.append(ev)
                    evs = list({(e[1].num): e for e in store_evs}.values())
                    if debug:
                        evs.append(("d", B_dbg.dsem, 16 * B_dbg.dcount))
                    T.wait_events("sp", evs)
        T.emit()
    return nc


def _slab_rows(j):
    v = np.arange(-4, 36)
    rows = 32 * j + v
    if j == 0:
        rows[0:4] = [4, 5, 6, 7]
    if j == 3:
        rows[36:40] = [120, 121, 122, 123]
    return rows


def _rope_tables(rows_actual):
    half = HD // 2
    inv = (10000.0 ** (-np.arange(0, half, 2, dtype=np.float32) / np.float32(half))).astype(np.float32)
    row = np.repeat(rows_actual.astype(np.float32), GW)
    col = np.tile(np.arange(GW, dtype=np.float32), len(rows_actual))
    ang_r = row[:, None] * inv
    ang_c = col[:, None] * inv
    ang = np.concatenate([ang_r, ang_r, ang_c, ang_c], axis=-1).astype(np.float32)
    cos = np.cos(ang).astype(np.float32)
    sin = np.sin(ang).astype(np.float32)
    sign = np.ones(HD, np.float32)
    sign[0:16] = -1.0
    sign[32:48] = -1.0
    sin = sin * sign[None, :]
    cosT = np.ascontiguousarray(np.concatenate([cos.T, cos.T], axis=0))
    sinT = np.ascontiguousarray(np.concatenate([sin.T, sin.T], axis=0))
    return cosT, sinT


def _bias_tables(rpb, j):
    R0 = 32 * j
    slab_rows = _slab_rows(j)
    tiles = [5, 0, 1, 14, 15]
    qr = np.arange(128) // 64
    qc = np.arange(128) % 64
    kr = np.arange(128) // 64
    kc = np.arange(128) % 64
    outb = np.full((NH, 128, 5, 5, 128), NEG, np.float32)
    cs = np.clip(qc - 8, 0, GW - 16)
    for vi, t in enumerate(tiles):
        rq = R0 + 2 * t + qr
        rs = np.clip(rq - 4, 0, 128 - 8)
        lo_v = 2 * t - 4
        hi_v = 2 * t + 5
        for jb in range(5):
            v = 2 * t + 2 * (jb - 2) + kr
            rk = slab_rows[v + 4]
            direct_v = rk - R0
            remapped = (v != direct_v)
            dup = remapped & (direct_v >= lo_v) & (direct_v <= hi_v)
            rowok = (rk[None, :] >= rs[:, None]) & (rk[None, :] < rs[:, None] + 8) & (~dup)[None, :]
            colok = (kc[None, :] >= cs[:, None]) & (kc[None, :] < cs[:, None] + 16)
            ok = rowok & colok
            dr = np.clip(rk[None, :] - rq[:, None] + 7, 0, 14)
            dc = np.clip(kc[None, :] - qc[:, None] + 15, 0, 30)
            vals = rpb[:, dr, dc]
            outb[:, :, vi, jb, :] = np.where(ok[None], vals, np.float32(NEG))
    ob = outb.reshape(8, 2, 128, 5, 5, 128).transpose(0, 5, 3, 1, 4, 2)
    return np.ascontiguousarray(ob).reshape(8, 128, 5 * 2 * 5 * 128)


def _prep_shared(inp):
    f = np.float32
    w_in = np.asarray(inp["w_in"][0], f)
    sh = {}
    sh["w_mod"] = np.ascontiguousarray(np.asarray(inp["w_mod"][0], f))
    b_mod = np.asarray(inp["b_mod"][0], f)
    bm = b_mod[:2048].reshape(16, 128).T
    sh["bmod2"] = np.ascontiguousarray(np.repeat(bm, 2, axis=1))
    sh["bgt"] = np.ascontiguousarray(np.broadcast_to(b_mod[2048:3072], (128, D)))
    pg = np.asarray(inp["pre_g"][0], f).reshape(8, 128).T
    sh["preg2"] = np.ascontiguousarray(np.repeat(pg, 2, axis=1))
    sh["postg"] = np.ascontiguousarray(np.broadcast_to(np.asarray(inp["post_g"][0], f), (128, D)))
    W = w_in.reshape(8, 128, 10, 8, 128)
    parts = [0, 1, 2, 3, 4, 5, 7]
    wm = W[:, :, parts, :, :].transpose(3, 1, 0, 2, 4)
    sh["wmain"] = np.ascontiguousarray(wm).reshape(8, 128, 8 * 7 * 128)
    wvv = W[:, :, 6, :, :].reshape(8, 128, 8 // VP, VP * 128).transpose(2, 1, 0, 3)
    sh["wv"] = np.ascontiguousarray(wvv).reshape(8 // VP, 128, 8 * VP * 128)
    wgg = W[:, :, 8:10, :, :].transpose(3, 1, 0, 2, 4)
    sh["wg"] = np.ascontiguousarray(wgg).reshape(8, 128, 8 * 2 * 128)
    for nm, key in (("woc", "w_out_conv"), ("woa", "w_out_attn")):
        w = np.asarray(inp[key][0], f).reshape(8, 128, 8, 128)
        sh[nm] = np.ascontiguousarray(w.transpose(2, 1, 0, 3)).reshape(8, 128, 8 * 128)
    w = np.asarray(inp["w_o"][0], f).reshape(8, 128, D)
    sh["wo"] = np.ascontiguousarray(w.transpose(1, 0, 2)).reshape(128, 8 * D)
    cw = np.asarray(inp["conv_w"][0], f).reshape(3, 8, 128)
    sh["convw"] = np.ascontiguousarray(cw.transpose(2, 1, 0)).reshape(128, 24)
    sh["convb"] = np.ascontiguousarray(np.asarray(inp["conv_b"][0], f).reshape(8, 128).T)
    perm = np.zeros((128, 128), f)
    for po in range(128):
        hb, d = divmod(po, 64)
        q4 = d // 16
        pin = hb * 64 + (d + 16 if q4 in (0, 2) else d - 16)
        perm[pin, po] = 1.0
    sh["rm"] = perm
    sh["ident"] = np.eye(128, dtype=f)
    return sh


def _prep_core(inp, sh, i, bias_cache):
    f = np.float32
    b, j = divmod(i, 4)
    x = np.asarray(inp["x"], f)
    rows = _slab_rows(j)
    tok = (rows[:, None] * GW + np.arange(GW)[None, :]).reshape(-1)
    m = dict(sh)
    xs = x[b, tok, :]
    m["xT"] = np.ascontiguousarray(xs.reshape(SLAB // 256, 256, 8, 128).transpose(0, 3, 2, 1)).reshape(
        SLAB // 256, 128, 8 * 256)
    m["xo"] = np.ascontiguousarray(x[b, 2048 * j:2048 * (j + 1), :])
    m["ctxT"] = np.ascontiguousarray(np.asarray(inp["ctx"], f)[b].reshape(CTX, 8, 128).transpose(2, 1, 0)).reshape(
        128, 8 * CTX)
    cv = np.stack([np.asarray(inp["c"], f)[b], np.asarray(inp["c_ctx"], f)], axis=1)
    m["cvec"] = np.ascontiguousarray(cv.reshape(8, 128, 2).transpose(1, 0, 2)).reshape(128, 16)
    fl = np.ones((128, 2), f)
    if j == 0:
        fl[:, 0] = 0.0
    if j == 3:
        fl[:, 1] = 0.0
    m["flags"] = fl
    if j not in bias_cache:
        bias_cache[j] = (_bias_tables(np.asarray(inp["rpb"][0], f), j),) + _rope_tables(rows)
    m["bias"], m["cosT"], m["sinT"] = bias_cache[j]
    return m


_NC_CACHE = {}


def kernel(**inputs):
    if "nc" not in _NC_CACHE:
        _NC_CACHE["nc"] = build_nc()
    nc = _NC_CACHE["nc"]
    sh = _prep_shared(inputs)
    cache = {}
    in_maps = [_prep_core(inputs, sh, i, cache) for i in range(NCORES)]
    res = run_bass_kernel_spmd(nc, in_maps, core_ids=list(range(NCORES)))
    outp = np.empty((2, SEQ, D), np.float32)
    for i in range(NCORES):
        b, j = divmod(i, 4)
        outp[b, 2048 * j:2048 * (j + 1), :] = res.results[i]["out"]
    return outp
```

```python
import numpy as np
from contextlib import ExitStack
import concourse.bass as bass
import concourse.mybir as mybir
from concourse.bass_utils import run_bass_kernel_spmd

F32 = mybir.dt.float32
BF16 = mybir.dt.bfloat16
ALU = mybir.AluOpType
AF = mybir.ActivationFunctionType

NCORES = 8
D = 1024
SEQ = 8192
GW = 64
CTX = 256
NH = 16
HD = 64
OWN = 2048
SLAB = 2560
KTW = SLAB + CTX
NEG = -30000.0
EPS = 1e-6
VP = 1
NVT = 22
ENGS = ("pe", "act", "dve", "pool", "sp")


class Buf:
    __slots__ = ("name", "lw", "rd", "dsem", "dcount")

    def __init__(self, name):
        self.name = name
        self.lw = None
        self.rd = []
        self.dsem = None
        self.dcount = 0


class Op:
    __slots__ = ("eng", "fn", "deps", "idx", "flag", "val", "waits", "dma_inc")

    def __init__(self, eng, fn):
        self.eng = eng
        self.fn = fn
        self.deps = []
        self.flag = False
        self.val = 0
        self.waits = []
        self.dma_inc = None


class Tracker:
    def __init__(self, nc, stack):
        self.nc = nc
        self.stack = stack
        self.ops = {e: [] for e in ENGS}
        self.esem = {e: stack.enter_context(nc.semaphore("es_" + e)) for e in ENGS}
        self.nsem = 0
        self.order = []
        self.dma_last = {}

    def new_sem(self, name):
        self.nsem += 1
        return self.stack.enter_context(self.nc.semaphore(f"ds_{name}_{self.nsem}"))

    def _collect(self, op, reads, writes):
        for b in reads:
            if b.lw is not None:
                op.deps.append(b.lw)
        for b in writes:
            if b.lw is not None:
                op.deps.append(b.lw)
            op.deps.extend(b.rd)

    def _add(self, o):
        o.idx = len(self.ops[o.eng])
        self.ops[o.eng].append(o)
        self.order.append(o)

    def op(self, eng, fn, reads=(), writes=()):
        o = Op(eng, fn)
        self._collect(o, reads, writes)
        self._add(o)
        ev = ("e", eng, o)
        for b in reads:
            b.rd.append(ev)
        for b in writes:
            b.lw = ev
            b.rd = []
        return o

    def dma(self, q, out, in_, reads=(), writes=(), sem_buf=None):
        sb = sem_buf if sem_buf is not None else (writes[0] if writes else reads[0])
        if sb.dsem is None:
            sb.dsem = self.new_sem(sb.name)
        sb.dcount += 1
        o = Op(q, lambda e, out=out, in_=in_: e.dma_start(out=out, in_=in_))
        self._collect(o, reads, writes)
        self._add(o)
        o.dma_inc = sb.dsem
        ev = ("d", sb.dsem, 16 * sb.dcount)
        self.dma_last[sb.dsem.num] = ev
        for b in reads:
            b.rd.append(ev)
        for b in writes:
            b.lw = ev
            b.rd = []
        return ev

    def wait_events(self, eng, evs):
        o = Op(eng, None)
        o.deps.extend(evs)
        self._add(o)

    def barrier(self):
        evs = list(self.dma_last.values())
        for e in ENGS:
            for o in reversed(self.ops[e]):
                if o.fn is not None and o.dma_inc is None:
                    evs.append(("e", e, o))
                    break
        for e in ENGS:
            self.wait_events(e, evs)

    def resolve(self):
        known = {e: {p: -1 for p in ENGS} for e in ENGS}
        dknown = {e: {} for e in ENGS}
        for o in self.order:
            need = {}
            dneed = {}
            for ev in o.deps:
                if ev[0] == "e":
                    _, p, po = ev
                    if p == "pe" and o.eng == "pe":
                        continue
                    if po.idx > need.get(p, (-1, None))[0]:
                        need[p] = (po.idx, po)
                else:
                    _, sem, val = ev
                    if val > dneed.get(sem.num, (0, None))[0]:
                        dneed[sem.num] = (val, sem)
            for p, (idx, po) in need.items():
                if known[o.eng][p] >= idx:
                    continue
                known[o.eng][p] = idx
                po.flag = True
                o.waits.append(("e", p, po))
            for num, (val, sem) in dneed.items():
                if dknown[o.eng].get(num, 0) >= val:
                    continue
                dknown[o.eng][num] = val
                o.waits.append(("d", sem, val))
        for e in ENGS:
            c = 0
            for o in self.ops[e]:
                if o.flag:
                    c += 1
                    o.val = c

    def emit(self):
        self.resolve()
        nc = self.nc
        with nc.Block() as block:
            def run(eng_name):
                def body(e):
                    for o in self.ops[eng_name]:
                        for w in o.waits:
                            if w[0] == "e":
                                e.wait_ge(self.esem[w[1]], w[2].val)
                            else:
                                e.wait_ge(w[1], w[2])
                        if o.fn is None:
                            continue
                        ins = o.fn(e)
                        if o.dma_inc is not None:
                            ins.then_inc(o.dma_inc, 16)
                        if o.flag:
                            ins.then_inc(self.esem[eng_name], 1)
                return body
            block.tensor(run("pe"))
            block.scalar(run("act"))
            block.vector(run("dve"))
            block.gpsimd(run("pool"))
            block.sync(run("sp"))


def build_nc(debug=False):
    nc = bass.Bass("TRN2", target_bir_lowering=False)

    def din(name, shape):
        return nc.dram_tensor(name, shape, F32, kind="ExternalInput").ap()

    xT = din("xT", [SLAB // 256, 128, 8 * 256])
    xo = din("xo", [OWN, D])
    ctxT = din("ctxT", [128, 8 * CTX])
    cvec = din("cvec", [128, 16])
    w_mod = din("w_mod", [D, 3 * D])
    bmod2 = din("bmod2", [128, 32])
    bgt = din("bgt", [128, D])
    preg2 = din("preg2", [128, 16])
    postg = din("postg", [128, D])
    wmain = din("wmain", [8, 128, 8 * 7 * 128])
    wv = din("wv", [8 // VP, 128, 8 * VP * 128])
    wg = din("wg", [8, 128, 8 * 2 * 128])
    woc = din("woc", [8, 128, 8 * 128])
    woa = din("woa", [8, 128, 8 * 128])
    wo = din("wo", [128, 8 * D])
    convw = din("convw", [128, 24])
    convb = din("convb", [128, 8])
    flags = din("flags", [128, 2])
    bias = din("bias", [8, 128, 5 * 2 * 5 * 128])
    cosT = din("cosT", [128, SLAB])
    sinT = din("sinT", [128, SLAB])
    rm = din("rm", [128, 128])
    ident = din("ident", [128, 128])
    out = nc.dram_tensor("out", [OWN, D], F32, kind="ExternalOutput").ap()
    dbg = {}
    if debug:
        for nm, shp in (("d_hT", [128, 8 * SLAB]), ("d_aT", [128, 8 * OWN]), ("d_cT", [128, 8 * OWN]),
                        ("d_mT", [128, 8 * OWN]), ("d_mod", [128, 32]), ("d_pgt", [128, D]),
                        ("d_qT", [128, OWN]), ("d_kT", [128, KTW])):
            dbg[nm] = nc.dram_tensor(nm, shp, F32, kind="ExternalOutput").ap()

    with ExitStack() as st:
        T = Tracker(nc, st)

        def sbt(stack, name, cols, dt):
            return stack.enter_context(nc.sbuf_tensor(name, [128, cols], dt))

        Q = [st.enter_context(nc.psum_tensor(f"Q{i}", [128, 1024], F32)) for i in range(4)]
        BK = [Buf(f"bk{i}") for i in range(8)]

        def bank(i):
            return Q[i // 2][:, (i % 2) * 512:(i % 2 + 1) * 512]

        rr = [0]

        def nextbank():
            i = rr[0] % 8
            rr[0] += 1
            return i

        def MM(o, l, r, start, stop, reads, writes):
            T.op("pe", lambda e, o=o, l=l, r=r, s=start, p=stop: e.matmul(o, lhsT=l, rhs=r, start=s, stop=p),
                 reads, writes)

        def ACT(o, i, func, reads, writes, bias=None, scale=None, accum=None):
            kw = {}
            if bias is not None:
                kw["bias"] = bias
            if scale is not None:
                kw["scale"] = scale
            if accum is not None:
                kw["accum_out"] = accum
            T.op("act", lambda e, o=o, i=i, f=func, kw=kw: e.activation(out=o, in_=i, func=f, **kw), reads, writes)

        def TTo(eng, o, a, b, op, reads, writes):
            T.op(eng, lambda e, o=o, a=a, b=b, op=op: e.tensor_tensor(out=o, in0=a, in1=b, op=op), reads, writes)

        def TS(eng, o, a, s1, s2, op0, op1, reads, writes):
            if s2 is None:
                T.op(eng, lambda e, o=o, a=a, s1=s1, op0=op0: e.tensor_scalar(out=o, in0=a, scalar1=s1, scalar2=None,
                                                                             op0=op0), reads, writes)
            else:
                T.op(eng, lambda e, o=o, a=a, s1=s1, s2=s2, op0=op0, op1=op1: e.tensor_scalar(
                    out=o, in0=a, scalar1=s1, scalar2=s2, op0=op0, op1=op1), reads, writes)

        def STT(eng, o, a, s, b, op0, op1, reads, writes):
            T.op(eng, lambda e, o=o, a=a, s=s, b=b, op0=op0, op1=op1: e.scalar_tensor_tensor(
                out=o, in0=a, scalar=s, in1=b, op0=op0, op1=op1), reads, writes)

        def CP(eng, o, i, reads, writes):
            T.op(eng, lambda e, o=o, i=i: e.tensor_copy(out=o, in_=i), reads, writes)

        def RECIP(o, i, reads, writes):
            T.op("dve", lambda e, o=o, i=i: e.reciprocal(out=o, in_=i), reads, writes)

        def MEMSET(eng, o, v, writes):
            T.op(eng, lambda e, o=o, v=v: e.memset(o, v), (), writes)

        def dump(name, src, B):
            if debug and name in dbg:
                T.dma("pool", dbg[name], src, reads=[B], sem_buf=B_dbg)

        B_dbg = Buf("dbg")

        ident_f = sbt(st, "ident_f", 128, F32); B_identf = Buf("identf")
        ident_b = sbt(st, "ident_b", 128, BF16); B_identb = Buf("identb")
        rm_b = sbt(st, "rm_b", 128, BF16); B_rm = Buf("rm")
        ones_b = sbt(st, "ones_b", 128, BF16); B_ones = Buf("ones")
        cos_s = sbt(st, "cos_s", SLAB, BF16); B_cos = Buf("cos")
        sin_s = sbt(st, "sin_s", SLAB, BF16); B_sin = Buf("sin")
        pgt = sbt(st, "pgt", D, F32); B_pgt = Buf("pgt")
        modsb = sbt(st, "modsb", 32, F32); B_mod = Buf("mod")
        gs = sbt(st, "gs", 16, F32); B_gs = Buf("gs")
        convw_s = sbt(st, "convw_s", 24, F32); B_cw = Buf("cw")
        convb_s = sbt(st, "convb_s", 8, F32); B_cb = Buf("cb")
        flags_s = sbt(st, "flags_s", 2, F32); B_fl = Buf("fl")
        Bc = Buf("consts")

        T.dma("sp", ident_f[:], ident, writes=[B_identf])
        T.dma("sp", convw_s[:], convw, writes=[B_cw])
        T.dma("sp", convb_s[:], convb, writes=[B_cb])
        T.dma("sp", flags_s[:], flags, writes=[B_fl])
        MEMSET("pool", ones_b[:], 1.0, [B_ones])

        with ExitStack() as sA:
            hT = sbt(sA, "hT", 8 * SLAB, BF16); B_hT = Buf("hT")
            hcT = sbt(sA, "hcT", 8 * CTX, BF16); B_hcT = Buf("hcT")
            aT = sbt(sA, "aT", 8 * OWN, BF16); B_aT = [Buf(f"aT{i}") for i in range(8)]
            cT = sbt(sA, "cT", 8 * OWN, BF16); B_cT = [Buf(f"cT{i}") for i in range(8)]
            wmc = sbt(sA, "wmc", 8 * 4 * 128, BF16); B_wmc = Buf("wmc")
            wv_s = sbt(sA, "wv_s", 8 * VP * 128, BF16); B_wv = Buf("wv")
            wmain_v = [wmain[cc].rearrange("p (kc j n) -> p kc j n", kc=8, j=7) for cc in range(8)]

            with ExitStack() as s0:
                cv = sbt(s0, "cv", 16, F32); B_cv = Buf("cv")
                sl = sbt(s0, "sl", 16, BF16); B_sl = Buf("sl")
                ones_f = sbt(s0, "ones_f", 128, F32); B_onesf = Buf("onesf")
                srep = sbt(s0, "srep", 8 * 128, BF16); B_srep = Buf("srep")
                wmbs = [sbt(s0, f"wmb{i}", 8 * 512, BF16) for i in range(2)]; B_wmbs = [Buf(f"wmb{i}") for i in range(2)]
                bmod_s = sbt(s0, "bmod_s", 32, F32); B_bm = Buf("bm")
                preg_s = sbt(s0, "preg_s", 16, F32); B_pg = Buf("pg")
                bgt_s = sbt(s0, "bgt_s", D, F32); B_bgt = Buf("bgt")
                postg_s = sbt(s0, "postg_s", D, F32); B_pog = Buf("pog")
                tmp16 = sbt(s0, "tmp16", 16, F32); B_t16 = Buf("t16")
                NG = 256
                xt = [sbt(s0, f"xt{i}", 8 * NG, F32) for i in range(4)]; B_xt = [Buf(f"xt{i}") for i in range(4)]
                sq = [sbt(s0, f"sq{i}", 8 * NG, BF16) for i in range(2)]; B_sq = [Buf(f"sq{i}") for i in range(2)]
                rstd = [sbt(s0, f"rstd{i}", NG, F32) for i in range(3)]; B_rstd = [Buf(f"rstd{i}") for i in range(3)]
                tmpx = [sbt(s0, f"tmpx{i}", NG, F32) for i in range(4)]; B_tmpx = [Buf(f"tmpx{i}") for i in range(4)]

                T.dma("sp", cv[:], cvec, writes=[B_cv])
                T.dma("sp", bmod_s[:], bmod2, writes=[B_bm])
                T.dma("sp", preg_s[:], preg2, writes=[B_pg])
                T.dma("sp", bgt_s[:], bgt, writes=[B_bgt])
                T.dma("sp", postg_s[:], postg, writes=[B_pog])
                wmr = w_mod.rearrange("(kc p) n -> p kc n", p=128)
                def load_wm(k):
                    T.dma("pool", wmbs[k % 2][:].rearrange("p (kc n) -> p kc n", kc=8), wmr[:, :, k * 512:(k + 1) * 512],
                          writes=[B_wmbs[k % 2]])
                load_wm(0)
                load_wm(1)
                T.dma("pool", ident_b[:], ident, writes=[B_identb])
                T.dma("pool", rm_b[:], rm, writes=[B_rm])
                T.dma("pool", cos_s[:], cosT, writes=[B_cos])
                T.dma("pool", sin_s[:], sinT, writes=[B_sin])
                MEMSET("pool", ones_f[:], 1.0, [B_onesf])
                ACT(sl[:], cv[:], AF.Silu, [B_cv], [B_sl])
                for kc in range(8):
                    TS("dve", srep[:, kc * 128:(kc + 1) * 128], ones_f[:], sl[:, kc * 2:kc * 2 + 1], None, ALU.mult, None,
                       [B_onesf, B_sl], [B_srep])

                groups = [(xT[g], g * NG, NG, hT, SLAB, g * NG, 0, B_hT) for g in range(SLAB // NG)]
                groups.append((ctxT, 0, CTX, hcT, CTX, 0, 1, B_hcT))

                stat_bank = {}
                eps_s = sbt(s0, "eps_s", 1, F32); B_eps = Buf("eps")
                MEMSET("pool", eps_s[:], EPS, [B_eps])

                def stats(gi):
                    src, t0, n, dst, dw, d0, tsel, Bd = groups[gi]
                    x_ = xt[gi % 4]
                    Bx = B_xt[gi % 4]
                    T.dma("sp", x_[:, 0:8 * n], src, writes=[Bx])
                    sq_ = sq[gi % 2]
                    ACT(sq_[:, 0:8 * n], x_[:, 0:8 * n], AF.Square, [Bx], [B_sq[gi % 2]])
                    bi = nextbank()
                    for kc in range(8):
                        MM(bank(bi)[:, 0:n], ones_b[:], sq_[:, kc * n:(kc + 1) * n], kc == 0, kc == 7,
                           [B_ones, B_sq[gi % 2]], [BK[bi]])
                    stat_bank[gi] = bi

                def fin(gi):
                    n = groups[gi][2]
                    bi = stat_bank[gi]
                    r_ = rstd[gi % 3]
                    Br = B_rstd[gi % 3]
                    ACT(r_[:, 0:n], bank(bi)[:, 0:n], AF.Ln, [BK[bi], B_eps], [Br], bias=eps_s[:, 0:1], scale=1.0 / D)
                    ACT(r_[:, 0:n], r_[:, 0:n], AF.Exp, [Br], [Br], scale=-0.5)

                def apply(gi):
                    src, t0, n, dst, dw, d0, tsel, Bd = groups[gi]
                    x_ = xt[gi % 4]
                    Bx = B_xt[gi % 4]
                    r_ = rstd[gi % 3]
                    Br = B_rstd[gi % 3]
                    for kc in range(8):
                        tx = tmpx[kc % 4]
                        Bt = B_tmpx[kc % 4]
                        STT("dve", tx[:, 0:n], x_[:, kc * n:(kc + 1) * n], gs[:, kc * 2 + tsel:kc * 2 + tsel + 1],
                            r_[:, 0:n], ALU.mult, ALU.mult, [Bx, B_gs, Br], [Bt])
                        ACT(dst[:, kc * dw + d0:kc * dw + d0 + n], tx[:, 0:n], AF.Identity, [Bt, B_mod], [Bd],
                            bias=modsb[:, kc * 2 + tsel:kc * 2 + tsel + 1], scale=1.0)

                stats(0)
                fin(0)
                bmod_i = nextbank()

                def modmm(j):
                    wi = (j // 4) % 2
                    off = (j % 4) * 128
                    for kc in range(8):
                        MM(bank(bmod_i)[:, j * 2:j * 2 + 2], wmbs[wi][:, kc * 512 + off:kc * 512 + off + 128],
                           sl[:, kc * 2:kc * 2 + 2], kc == 0, kc == 7, [B_wmbs[wi], B_sl], [BK[bmod_i]])
                for pk in range(4):
                    for j in range(pk * 4, pk * 4 + 4):
                        modmm(j)
                    load_wm(pk + 2)
                    if pk == 0:
                        stats(1)
                        fin(1)
                T.dma("pool", wv_s[:], wv[0], writes=[B_wv])
                T.dma("pool", wmc[:].rearrange("p (kc j n) -> p kc j n", kc=8, j=4), wmain_v[0][:, :, 0:4, :],
                      writes=[B_wmc])
                TTo("dve", modsb[:], bank(bmod_i)[:, 0:32], bmod_s[:], ALU.add, [BK[bmod_i], B_bm], [B_mod])
                TS("dve", tmp16[:], modsb[:, 16:32], 1.0, None, ALU.add, None, [B_mod], [B_t16])
                TTo("dve", gs[:], tmp16[:], preg_s[:], ALU.mult, [B_t16, B_pg], [B_gs])
                ng = len(groups)
                for gi in range(ng):
                    if gi + 2 < ng:
                        stats(gi + 2)
                    apply(gi)
                    if gi + 2 < ng:
                        fin(gi + 2)
                for half in range(2):
                    bi = nextbank()
                    for kc in range(8):
                        MM(bank(bi), srep[:, kc * 128:(kc + 1) * 128],
                           wmbs[half][:, kc * 512:(kc + 1) * 512],
                           kc == 0, kc == 7, [B_srep, B_wmbs[half]], [BK[bi]])
                    TTo("dve", pgt[:, half * 512:(half + 1) * 512], bank(bi), bgt_s[:, half * 512:(half + 1) * 512],
                        ALU.add, [BK[bi], B_bgt], [B_pgt])
                TTo("dve", pgt[:], pgt[:], postg_s[:], ALU.mult, [B_pgt, B_pog], [B_pgt])
                if debug:
                    dump("d_pgt", pgt[:], B_pgt)
                    dump("d_hT", hT[:], B_hT)
                    dump("d_mod", modsb[:], B_mod)
                T.barrier()

            with ExitStack() as s2:
                wma = sbt(s2, "wma", 8 * 3 * 128, BF16); B_wma = Buf("wma")
                V_s = sbt(s2, "V_s", NVT * VP * 2 * 65, BF16); B_V = Buf("V")
                u_s = sbt(s2, "u_s", OWN + 2, BF16); B_u = Buf("u")
                bz_s = sbt(s2, "bz_s", OWN, BF16); B_bz = Buf("bz")
                acc = [sbt(s2, f"acc{i}", 512, F32) for i in range(2)]; B_acc = [Buf(f"acc{i}") for i in range(2)]
                xi_s = sbt(s2, "xi_s", 512, F32); B_xi = Buf("xi")
                sz_s = sbt(s2, "sz_s", 512, F32); B_sz = Buf("sz")
                hal_s = sbt(s2, "hal_s", 4, F32); B_hal = Buf("hal")
                qTz = [sbt(s2, f"qTz{i}", OWN, BF16) for i in range(2)]; B_qT = Buf("qT")
                kT = sbt(s2, "kT", KTW, BF16); B_kT = Buf("kT")
                szb = sbt(s2, "szb", OWN, BF16); B_szb = Buf("szb")
                qraw = sbt(s2, "qraw", 512, BF16); B_qraw = Buf("qraw")
                t1 = sbt(s2, "t1", 512, F32); B_t1 = Buf("t1")
                t2 = sbt(s2, "t2", 512, F32); B_t2 = Buf("t2")
                bias_s = sbt(s2, "bias_s", 5 * 2 * 5 * 128, BF16); B_bias = Buf("bias")
                PT = [[sbt(s2, f"PT{i}_{h}", 896, BF16) for h in range(2)] for i in range(2)]
                B_PT = [[Buf(f"PT{i}_{h}") for h in range(2)] for i in range(2)]
                print("SBUF bytes remaining at phase-2 peak:", nc.sbuf_bytes_remaining)
                attn_n = [sbt(s2, f"attn_n{i}", 128, F32) for i in range(2)]; B_an = [Buf(f"an{i}") for i in range(2)]
                rden = [sbt(s2, f"rden{i}", 2, F32) for i in range(2)]; B_rd = [Buf(f"rd{i}") for i in range(2)]

                MEMSET("pool", V_s[:], 1.0, [B_V])
                for i in range(2):
                    MEMSET("pool", qTz[i][:], 0.0, [B_qT])

                def load_wmc(c):
                    T.dma("pool", wmc[:].rearrange("p (kc j n) -> p kc j n", kc=8, j=4), wmain_v[c][:, :, 0:4, :],
                          writes=[B_wmc])

                def load_wma(c):
                    T.dma("pool", wma[:].rearrange("p (kc j n) -> p kc j n", kc=8, j=3), wmain_v[c][:, :, 4:7, :],
                          writes=[B_wma])

                def load_bias(c):
                    T.dma("pool", bias_s[:], bias[c], writes=[B_bias])

                def load_wv(c):
                    T.dma("pool", wv_s[:], wv[c // VP], writes=[B_wv])

                load_wma(0)
                load_bias(0)
                for cc in range(8):
                    pl = cc % VP
                    if pl == 0:
                        for vt in range(NVT):
                            bi = nextbank()
                            for kc in range(8):
                                if vt < 20:
                                    l = hT[:, kc * SLAB + vt * 128:kc * SLAB + (vt + 1) * 128]
                                    rdl = [B_hT, B_wv]
                                else:
                                    l = hcT[:, kc * CTX + (vt - 20) * 128:kc * CTX + (vt - 19) * 128]
                                    rdl = [B_hcT, B_wv]
                                MM(bank(bi)[:, 0:VP * 128], l, wv_s[:, kc * VP * 128:(kc + 1) * VP * 128],
                                   kc == 0, kc == 7, rdl, [BK[bi]])
                            ov = V_s[:, vt * VP * 130:(vt + 1) * VP * 130].rearrange("p (h c) -> p h c", c=65)[:, :, 0:64]
                            iv = bank(bi)[:, 0:VP * 128].rearrange("p (h c) -> p h c", c=64)
                            ACT(ov, iv, AF.Copy, [BK[bi]], [B_V])
                        if cc + VP < 8:
                            load_wv(cc + VP)

                    def wblk(w, nj, kc, j):
                        return w[:, (kc * nj + j) * 128:(kc * nj + j + 1) * 128]

                    for g in range(4):
                        tok0 = 256 + g * 512
                        bs = [nextbank() for _ in range(4)]
                        for j in range(4):
                            for kc in range(8):
                                MM(bank(bs[j]), wblk(wmc, 4, kc, j), hT[:, kc * SLAB + tok0:kc * SLAB + tok0 + 512],
                                   kc == 0, kc == 7, [B_wmc, B_hT], [BK[bs[j]]])
                        ACT(xi_s[:], bank(bs[2]), AF.Copy, [BK[bs[2]]], [B_xi])
                        TTo("dve", u_s[:, 1 + g * 512:1 + (g + 1) * 512], bank(bs[1]), xi_s[:], ALU.mult,
                            [BK[bs[1]], B_xi], [B_u])
                        ACT(sz_s[:], bank(bs[3]), AF.Silu, [BK[bs[3]]], [B_sz])
                        TTo("dve", bz_s[:, g * 512:(g + 1) * 512], bank(bs[0]), sz_s[:], ALU.mult,
                            [BK[bs[0]], B_sz], [B_bz])
                    bi = nextbank()
                    for hi, tk in enumerate((255, 2304)):
                        for jj, j in enumerate((1, 2)):
                            col = hi * 2 + jj
                            for kc in range(8):
                                MM(bank(bi)[:, col:col + 1], wblk(wmc, 4, kc, j), hT[:, kc * SLAB + tk:kc * SLAB + tk + 1],
                                   kc == 0, kc == 7, [B_wmc, B_hT], [BK[bi]])
                    ACT(hal_s[:], bank(bi)[:, 0:4], AF.Copy, [BK[bi]], [B_hal])
                    for hi, ucol in enumerate((0, OWN + 1)):
                        STT("dve", u_s[:, ucol:ucol + 1], hal_s[:, hi * 2:hi * 2 + 1], flags_s[:, hi:hi + 1],
                            hal_s[:, hi * 2 + 1:hi * 2 + 2], ALU.mult, ALU.mult, [B_hal, B_fl], [B_u])
                    for pc in range(4):
                        s_ = pc * 512
                        a_ = acc[pc % 2]
                        Ba = B_acc[pc % 2]
                        TS("dve", a_[:], u_s[:, 1 + s_:1 + s_ + 512], convw_s[:, cc * 3 + 1:cc * 3 + 2],
                           convb_s[:, cc:cc + 1], ALU.mult, ALU.add, [B_u, B_cw, B_cb], [Ba])
                        STT("dve", a_[:], u_s[:, s_:s_ + 512], convw_s[:, cc * 3:cc * 3 + 1], a_[:], ALU.mult, ALU.add,
                            [B_u, B_cw, Ba], [Ba])
                        STT("dve", a_[:], u_s[:, 2 + s_:2 + s_ + 512], convw_s[:, cc * 3 + 2:cc * 3 + 3], a_[:],
                            ALU.mult, ALU.add, [B_u, B_cw, Ba], [Ba])
                        TTo("dve", aT[:, cc * OWN + s_:cc * OWN + s_ + 512], a_[:], bz_s[:, s_:s_ + 512], ALU.mult,
                            [Ba, B_bz], [B_aT[cc]])

                    if cc + 1 < 8:
                        load_wmc(cc + 1)

                    def rope_p1(j, tok0, scl):
                        ba = nextbank()
                        for kc in range(8):
                            MM(bank(ba), wblk(wma, 3, kc, j), hT[:, kc * SLAB + tok0:kc * SLAB + tok0 + 512],
                               kc == 0, kc == 7, [B_wma, B_hT], [BK[ba]])
                        ACT(qraw[:], bank(ba), AF.Copy, [BK[ba]], [B_qraw], scale=scl)
                        STT("dve", t1[:], bank(ba), scl, cos_s[:, tok0:tok0 + 512], ALU.mult, ALU.mult,
                            [BK[ba], B_cos, B_qraw], [B_t1])

                    def rope_p2(tok0, dst, d0, Bd):
                        bb = nextbank()
                        MM(bank(bb), rm_b[:], qraw[:], True, True, [B_rm, B_qraw], [BK[bb]])
                        TTo("dve", t2[:], bank(bb), sin_s[:, tok0:tok0 + 512], ALU.mult, [BK[bb], B_sin], [B_t2])
                        if dst is None:
                            for hh in range(2):
                                TTo("dve", qTz[hh][hh * 64:(hh + 1) * 64, d0:d0 + 512], t1[hh * 64:(hh + 1) * 64, :],
                                    t2[hh * 64:(hh + 1) * 64, :], ALU.add, [B_t1, B_t2], [Bd])
                        else:
                            TTo("dve", dst[:, d0:d0 + 512], t1[:], t2[:], ALU.add, [B_t1, B_t2], [Bd])

                    def zb_group(g):
                        ba = nextbank()
                        tok0 = 256 + g * 512
                        for kc in range(8):
                            MM(bank(ba), wblk(wma, 3, kc, 2), hT[:, kc * SLAB + tok0:kc * SLAB + tok0 + 512],
                               kc == 0, kc == 7, [B_wma, B_hT], [BK[ba]])
                        ACT(szb[:, g * 512:(g + 1) * 512], bank(ba), AF.Silu, [BK[ba]], [B_szb])

                    def ctxk_group():
                        ba = nextbank()
                        for kc in range(8):
                            MM(bank(ba)[:, 0:CTX], wblk(wma, 3, kc, 1), hcT[:, kc * CTX:(kc + 1) * CTX], kc == 0, kc == 7,
                               [B_wma, B_hcT], [BK[ba]])
                        ACT(kT[:, SLAB:KTW], bank(ba)[:, 0:CTX], AF.Copy, [BK[ba]], [B_kT])

                    fillers = [lambda g=g: zb_group(g) for g in range(4)] + [ctxk_group]
                    rjobs = [(0, 256 + g * 512, None, g * 512, B_qT, 0.125) for g in range(4)]
                    rjobs += [(1, g * 512, kT, g * 512, B_kT, 1.0) for g in range(5)]
                    for i, (j, tok0, dst, d0, Bd, scl) in enumerate(rjobs):
                        rope_p1(j, tok0, scl)
                        if i < len(fillers):
                            fillers[i]()
                        rope_p2(tok0, dst, d0, Bd)
                    if cc + 1 < 8:
                        load_wma(cc + 1)

                    def emit_S(t):
                        var = {0: 1, 1: 2, 14: 3, 15: 4}.get(t, 0)
                        for hl in range(2):
                            S = Q[hl]
                            BS = [BK[2 * hl], BK[2 * hl + 1]]
                            qv = qTz[hl][:, t * 128:(t + 1) * 128]
                            for jb in range(5):
                                kt0 = (t + jb) * 128
                                Bb = BS[jb // 4]
                                MM(S[:, jb * 128:(jb + 1) * 128], kT[:, kt0:kt0 + 128], qv, True, False,
                                   [B_kT, B_qT], [Bb])
                                bo = ((var * 2 + hl) * 5 + jb) * 128
                                MM(S[:, jb * 128:(jb + 1) * 128], bias_s[:, bo:bo + 128], ident_b[:], False, True,
                                   [B_bias, B_identb], [Bb])
                            for cb in range(2):
                                MM(S[:, (5 + cb) * 128:(6 + cb) * 128], kT[:, SLAB + cb * 128:SLAB + (cb + 1) * 128],
                                   qv, True, True, [B_kT, B_qT], [BS[1]])
                            ACT(PT[t % 2][hl][:, 0:896], S[:, 0:896], AF.Exp, [BS[0], BS[1]], [B_PT[t % 2][hl]])

                    def emit_PV(t):
                        ob = 4 + (t % 2)
                        for hl in range(2):
                            for blk in range(7):
                                vt = (t + blk) if blk < 5 else (20 + blk - 5)
                                vo = ((vt * VP + pl) * 2 + hl) * 65
                                MM(bank(ob)[:, hl * 65:(hl + 1) * 65], PT[t % 2][hl][:, blk * 128:(blk + 1) * 128],
                                   V_s[:, vo:vo + 65], blk == 0, blk == 6, [B_PT[t % 2][hl], B_V], [BK[ob]])

                    def emit_N(t):
                        ob = 4 + (t % 2)
                        rd_ = rden[t % 2]
                        an_ = attn_n[t % 2]
                        for hl in range(2):
                            RECIP(rd_[:, hl:hl + 1], bank(ob)[:, hl * 65 + 64:hl * 65 + 65], [BK[ob]], [B_rd[t % 2]])
                        for hl in range(2):
                            TS("dve", an_[:, hl * 64:(hl + 1) * 64], bank(ob)[:, hl * 65:hl * 65 + 64], rd_[:, hl:hl + 1],
                               None, ALU.mult, None, [BK[ob], B_rd[t % 2]], [B_an[t % 2]])

                    def emit_X(t):
                        tb = 6 + (t % 2)
                        an_ = attn_n[t % 2]
                        T.op("pe", lambda e, o=bank(tb)[:, 0:128], i=an_[:], idn=ident_f[:]: e.transpose(o, i, idn),
                             [B_an[t % 2], B_identf], [BK[tb]])
                        TTo("dve", cT[:, cc * OWN + t * 128:cc * OWN + (t + 1) * 128], bank(tb)[:, 0:128],
                            szb[:, t * 128:(t + 1) * 128], ALU.mult, [BK[tb], B_szb], [B_cT[cc]])

                    emit_S(0)
                    for t in range(16):
                        if t + 1 < 16:
                            emit_S(t + 1)
                        emit_PV(t)
                        if t > 0:
                            emit_X(t - 1)
                        emit_N(t)
                    emit_X(15)
                    if cc + 1 < 8:
                        load_bias(cc + 1)
                if debug:
                    dump("d_aT", aT[:], B_aT[7])
                    dump("d_cT", cT[:], B_cT[7])
                T.barrier()

            with ExitStack() as s3:
                mT = sbt(s3, "mT", 8 * OWN, BF16); B_mT = [Buf(f"mT{i}") for i in range(8)]
                wo_s = sbt(s3, "wo_s", 8 * D, BF16); B_wo = Buf("wo")
                T.dma("pool", wo_s[:], wo, writes=[B_wo])
                with ExitStack() as s3b:
                    w3 = [[sbt(s3b, f"w3_{i}_{k}", 8 * 128, BF16) for k in range(2)] for i in range(2)]
                    B_w3 = [[Buf(f"w3_{i}_{k}") for k in range(2)] for i in range(2)]
                    wg_s = [sbt(s3b, f"wg_s{i}", 8 * 2 * 128, BF16) for i in range(2)]; B_wg = [Buf(f"wg{i}") for i in range(2)]
                    sg = [sbt(s3b, f"sg{i}", 512, F32) for i in range(2)]; B_sg = [Buf(f"sg{i}") for i in range(2)]
                    m12 = [sbt(s3b, f"m12_{i}", 512, F32) for i in range(2)]; B_m12 = [Buf(f"m12_{i}") for i in range(2)]
                    def load_w3(o):
                        k = o % 2
                        T.dma("pool", w3[k][0][:], woc[o], writes=[B_w3[k][0]])
                        T.dma("pool", w3[k][1][:], woa[o], writes=[B_w3[k][1]])
                        T.dma("pool", wg_s[k][:], wg[o], writes=[B_wg[k]])
                    load_w3(0)
                    for oc in range(8):
                        sl_ = oc % 2
                        if oc + 1 < 8:
                            load_w3(oc + 1)
                        for g in range(4):
                            tok0 = 256 + g * 512
                            bs = [nextbank() for _ in range(4)]
                            for chc in range(8):
                                MM(bank(bs[0]), w3[sl_][0][:, chc * 128:(chc + 1) * 128],
                                   aT[:, chc * OWN + g * 512:chc * OWN + (g + 1) * 512], chc == 0, chc == 7,
                                   [B_w3[sl_][0], B_aT[chc]], [BK[bs[0]]])
                            for chc in range(8):
                                MM(bank(bs[1]), w3[sl_][1][:, chc * 128:(chc + 1) * 128],
                                   cT[:, chc * OWN + g * 512:chc * OWN + (g + 1) * 512], chc == 0, chc == 7,
                                   [B_w3[sl_][1], B_cT[chc]], [BK[bs[1]]])
                            for j in range(2):
                                for kc in range(8):
                                    MM(bank(bs[2 + j]), wg_s[sl_][:, (kc * 2 + j) * 128:(kc * 2 + j + 1) * 128],
                                       hT[:, kc * SLAB + tok0:kc * SLAB + tok0 + 512], kc == 0, kc == 7,
                                       [B_wg[sl_], B_hT], [BK[bs[2 + j]]])
                            for j in range(2):
                                ACT(sg[j][:], bank(bs[2 + j]), AF.Sigmoid, [BK[bs[2 + j]]], [B_sg[j]])
                                TTo("dve", m12[j][:], bank(bs[j]), sg[j][:], ALU.mult, [BK[bs[j]], B_sg[j]], [B_m12[j]])
                            TTo("dve", mT[:, oc * OWN + g * 512:oc * OWN + (g + 1) * 512], m12[0][:], m12[1][:], ALU.add,
                                [B_m12[0], B_m12[1]], [B_mT[oc]])
                    if debug:
                        dump("d_mT", mT[:], B_mT[7])
                    T.barrier()

                with ExitStack() as s4:
                    xtile = [sbt(s4, f"xtile{i}", D, F32) for i in range(2)]; B_xtile = [Buf(f"xtile{i}") for i in range(2)]
                    otile = [sbt(s4, f"otile{i}", D, F32) for i in range(2)]; B_ot = [Buf(f"ot{i}") for i in range(2)]
                    yg = [sbt(s4, f"yg{i}", D, F32) for i in range(2)]; B_yg = [Buf(f"yg{i}") for i in range(2)]
                    junk = sbt(s4, "junk", 512, F32); B_junk = Buf("junk")
                    ssq = [sbt(s4, f"ssq{i}", 4, F32) for i in range(2)]; B_ssq = [Buf(f"ssq{i}") for i in range(2)]
                    store_evs = []
                    T.dma("sp", xtile[0][:], xo[0:128, :], writes=[B_xtile[0]])
                    for t in range(16):
                        k_ = t % 2
                        if t + 1 < 16:
                            T.dma("sp", xtile[1 - k_][:], xo[(t + 1) * 128:(t + 2) * 128, :], writes=[B_xtile[1 - k_]])
                        qi = t % 4
                        Y = Q[qi]
                        BY = [BK[2 * qi], BK[2 * qi + 1]]
                        for half in range(2):
                            for oc in range(8):
                                MM(Y[:, half * 512:(half + 1) * 512], mT[:, oc * OWN + t * 128:oc * OWN + (t + 1) * 128],
                                   wo_s[:, oc * D + half * 512:oc * D + (half + 1) * 512], oc == 0, oc == 7,
                                   [B_mT[oc], B_wo], [BY[half]])
                        s_ = ssq[k_]
                        Bs = B_ssq[k_]
                        for half in range(2):
                            ACT(junk[:], Y[:, half * 512:(half + 1) * 512], AF.Square, [BY[half], Bs], [B_junk, Bs],
                                accum=s_[:, half:half + 1])
                        TTo("dve", s_[:, 2:3], s_[:, 0:1], s_[:, 1:2], ALU.add, [Bs], [Bs])
                        TS("dve", s_[:, 2:3], s_[:, 2:3], 1.0 / D, EPS, ALU.mult, ALU.add, [Bs], [Bs])
                        ACT(s_[:, 2:3], s_[:, 2:3], AF.Sqrt, [Bs], [Bs])
                        RECIP(s_[:, 3:4], s_[:, 2:3], [Bs], [Bs])
                        for half in range(2):
                            TTo("dve", yg[k_][:, half * 512:(half + 1) * 512], Y[:, half * 512:(half + 1) * 512],
                                pgt[:, half * 512:(half + 1) * 512], ALU.mult, [BY[half], B_pgt], [B_yg[k_]])
                        STT("dve", otile[k_][:], yg[k_][:], s_[:, 3:4], xtile[k_][:], ALU.mult, ALU.add,
                            [B_yg[k_], Bs, B_xtile[k_]], [B_ot[k_]])
                        ev = T.dma("sp", out[t * 128:(t + 1) * 128, :], otile[k_][:], reads=[B_ot[k_]])
                        store_evs.append(ev)
                    evs = list({(e[1].num): e for e in store_evs}.values())
                    if debug:
                        evs.append(("d", B_dbg.dsem, 16 * B_dbg.dcount))
                    T.wait_events("sp", evs)
        T.emit()
    return nc


def _slab_rows(j):
    v = np.arange(-4, 36)
    rows = 32 * j + v
    if j == 0:
        rows[0:4] = [4, 5, 6, 7]
    if j == 3:
        rows[36:40] = [120, 121, 122, 123]
    return rows


def _rope_tables(rows_actual):
    half = HD // 2
    inv = (10000.0 ** (-np.arange(0, half, 2, dtype=np.float32) / np.float32(half))).astype(np.float32)
    row = np.repeat(rows_actual.astype(np.float32), GW)
    col = np.tile(np.arange(GW, dtype=np.float32), len(rows_actual))
    ang_r = row[:, None] * inv
    ang_c = col[:, None] * inv
    ang = np.concatenate([ang_r, ang_r, ang_c, ang_c], axis=-1).astype(np.float32)
    cos = np.cos(ang).astype(np.float32)
    sin = np.sin(ang).astype(np.float32)
    sign = np.ones(HD, np.float32)
    sign[0:16] = -1.0
    sign[32:48] = -1.0
    sin = sin * sign[None, :]
    cosT = np.ascontiguousarray(np.concatenate([cos.T, cos.T], axis=0))
    sinT = np.ascontiguousarray(np.concatenate([sin.T, sin.T], axis=0))
    return cosT, sinT


def _bias_tables(rpb, j):
    R0 = 32 * j
    slab_rows = _slab_rows(j)
    tiles = [5, 0, 1, 14, 15]
    qr = np.arange(128) // 64
    qc = np.arange(128) % 64
    kr = np.arange(128) // 64
    kc = np.arange(128) % 64
    outb = np.full((NH, 128, 5, 5, 128), NEG, np.float32)
    cs = np.clip(qc - 8, 0, GW - 16)
    for vi, t in enumerate(tiles):
        rq = R0 + 2 * t + qr
        rs = np.clip(rq - 4, 0, 128 - 8)
        lo_v = 2 * t - 4
        hi_v = 2 * t + 5
        for jb in range(5):
            v = 2 * t + 2 * (jb - 2) + kr
            rk = slab_rows[v + 4]
            direct_v = rk - R0
            remapped = (v != direct_v)
            dup = remapped & (direct_v >= lo_v) & (direct_v <= hi_v)
            rowok = (rk[None, :] >= rs[:, None]) & (rk[None, :] < rs[:, None] + 8) & (~dup)[None, :]
            colok = (kc[None, :] >= cs[:, None]) & (kc[None, :] < cs[:, None] + 16)
            ok = rowok & colok
            dr = np.clip(rk[None, :] - rq[:, None] + 7, 0, 14)
            dc = np.clip(kc[None, :] - qc[:, None] + 15, 0, 30)
            vals = rpb[:, dr, dc]
            outb[:, :, vi, jb, :] = np.where(ok[None], vals, np.float32(NEG))
    ob = outb.reshape(8, 2, 128, 5, 5, 128).transpose(0, 2, 3, 1, 4, 5)
    return np.ascontiguousarray(ob).reshape(8, 128, 5 * 2 * 5 * 128)


def _prep_shared(inp):
    f = np.float32
    w_in = np.asarray(inp["w_in"][0], f)
    sh = {}
    sh["w_mod"] = np.ascontiguousarray(np.asarray(inp["w_mod"][0], f))
    b_mod = np.asarray(inp["b_mod"][0], f)
    bm = b_mod[:2048].reshape(16, 128).T
    sh["bmod2"] = np.ascontiguousarray(np.repeat(bm, 2, axis=1))
    sh["bgt"] = np.ascontiguousarray(np.broadcast_to(b_mod[2048:3072], (128, D)))
    pg = np.asarray(inp["pre_g"][0], f).reshape(8, 128).T
    sh["preg2"] = np.ascontiguousarray(np.repeat(pg, 2, axis=1))
    sh["postg"] = np.ascontiguousarray(np.broadcast_to(np.asarray(inp["post_g"][0], f), (128, D)))
    W = w_in.reshape(8, 128, 10, 8, 128)
    parts = [0, 1, 2, 3, 4, 5, 7]
    wm = W[:, :, parts, :, :].transpose(3, 1, 0, 2, 4)
    sh["wmain"] = np.ascontiguousarray(wm).reshape(8, 128, 8 * 7 * 128)
    wvv = W[:, :, 6, :, :].reshape(8, 128, 8 // VP, VP * 128).transpose(2, 1, 0, 3)
    sh["wv"] = np.ascontiguousarray(wvv).reshape(8 // VP, 128, 8 * VP * 128)
    wgg = W[:, :, 8:10, :, :].transpose(3, 1, 0, 2, 4)
    sh["wg"] = np.ascontiguousarray(wgg).reshape(8, 128, 8 * 2 * 128)
    for nm, key in (("woc", "w_out_conv"), ("woa", "w_out_attn")):
        w = np.asarray(inp[key][0], f).reshape(8, 128, 8, 128)
        sh[nm] = np.ascontiguousarray(w.transpose(2, 1, 0, 3)).reshape(8, 128, 8 * 128)
    w = np.asarray(inp["w_o"][0], f).reshape(8, 128, D)
    sh["wo"] = np.ascontiguousarray(w.transpose(1, 0, 2)).reshape(128, 8 * D)
    cw = np.asarray(inp["conv_w"][0], f).reshape(3, 8, 128)
    sh["convw"] = np.ascontiguousarray(cw.transpose(2, 1, 0)).reshape(128, 24)
    sh["convb"] = np.ascontiguousarray(np.asarray(inp["conv_b"][0], f).reshape(8, 128).T)
    perm = np.zeros((128, 128), f)
    for po in range(128):
        hb, d = divmod(po, 64)
        q4 = d // 16
        pin = hb * 64 + (d + 16 if q4 in (0, 2) else d - 16)
        perm[pin, po] = 1.0
    sh["rm"] = perm
    sh["ident"] = np.eye(128, dtype=f)
    return sh


def _prep_core(inp, sh, i, bias_cache):
    f = np.float32
    b, j = divmod(i, 4)
    x = np.asarray(inp["x"], f)
    rows = _slab_rows(j)
    tok = (rows[:, None] * GW + np.arange(GW)[None, :]).reshape(-1)
    m = dict(sh)
    xs = x[b, tok, :]
    m["xT"] = np.ascontiguousarray(xs.reshape(SLAB // 256, 256, 8, 128).transpose(0, 3, 2, 1)).reshape(
        SLAB // 256, 128, 8 * 256)
    m["xo"] = np.ascontiguousarray(x[b, 2048 * j:2048 * (j + 1), :])
    m["ctxT"] = np.ascontiguousarray(np.asarray(inp["ctx"], f)[b].reshape(CTX, 8, 128).transpose(2, 1, 0)).reshape(
        128, 8 * CTX)
    cv = np.stack([np.asarray(inp["c"], f)[b], np.asarray(inp["c_ctx"], f)], axis=1)
    m["cvec"] = np.ascontiguousarray(cv.reshape(8, 128, 2).transpose(1, 0, 2)).reshape(128, 16)
    fl = np.ones((128, 2), f)
    if j == 0:
        fl[:, 0] = 0.0
    if j == 3:
        fl[:, 1] = 0.0
    m["flags"] = fl
    if j not in bias_cache:
        bias_cache[j] = (_bias_tables(np.asarray(inp["rpb"][0], f), j),) + _rope_tables(rows)
    m["bias"], m["cosT"], m["sinT"] = bias_cache[j]
    return m


_NC_CACHE = {}


def kernel(**inputs):
    if "nc" not in _NC_CACHE:
        _NC_CACHE["nc"] = build_nc()
    nc = _NC_CACHE["nc"]
    sh = _prep_shared(inputs)
    cache = {}
    in_maps = [_prep_core(inputs, sh, i, cache) for i in range(NCORES)]
    res = run_bass_kernel_spmd(nc, in_maps, core_ids=list(range(NCORES)))
    outp = np.empty((2, SEQ, D), np.float32)
    for i in range(NCORES):
        b, j = divmod(i, 4)
        outp[b, 2048 * j:2048 * (j + 1), :] = res.results[i]["out"]
    return outp
```
